# Optimizing a Trainium2 kernel written in Bass

```python
import math
import jax
import jax.numpy as jnp
from jax import lax
import numpy as np


D_MODEL = 1024
BATCH = 16
SEQ = 2048
DEPTH = 2

CHUNK = 64
D_PLE = 256
N_BRANCH = 4
D_BRANCH = D_MODEL // 4
D_FF = 4 * D_MODEL
EPS = 1e-6
NEG_INF = -1e30

HG_HEADS = 4
HG_DK = D_BRANCH // HG_HEADS
HG_DV = D_BRANCH // HG_HEADS

SSD_HEADS = 4
SSD_HEADDIM = D_BRANCH // SSD_HEADS
SSD_GROUPS = 2
SSD_DSTATE = 64
SSD_CONV = 4
SSD_INNER = D_BRANCH
SSD_XBC = SSD_INNER + 2 * SSD_GROUPS * SSD_DSTATE

S5_GROUP_CH = 16
S5_GROUPS = D_BRANCH // S5_GROUP_CH
S5_STATE = 64

ATT_HEADS = 4
ATT_HEADDIM = D_BRANCH // ATT_HEADS
ATT_LEFT_CHUNKS = 8
ATT_BAND = (ATT_LEFT_CHUNKS + 1) * CHUNK
ATT_MAX_REL = 128

SPLIT_SIZES = (
    4 * D_BRANCH,
    SSD_INNER + SSD_XBC + SSD_HEADS,
    D_BRANCH,
    3 * D_BRANCH,
    N_BRANCH * D_MODEL,
)
N_IN = sum(SPLIT_SIZES)

kernel_name = 'hybrid_gated_branch_streaming_encoder'


def rmsnorm(x, g):
    xf = x.astype(jnp.float32)
    y = xf * lax.rsqrt(jnp.mean(xf * xf, axis=-1, keepdims=True) + EPS)
    return (y * g.astype(jnp.float32)).astype(x.dtype)


def split_cols(t, sizes):
    offsets = np.cumsum(np.asarray(sizes))[:-1].tolist()
    return jnp.split(t, offsets, axis=-1)


def to_chunks(t, n_heads, d):
    b_, s_ = t.shape[:2]
    return t.reshape(b_, s_ // CHUNK, CHUNK, n_heads, d).transpose(0, 3, 1, 2, 4)


def hgrn2_mixer(q, f_logit, i_in, g, lb, o_gain):
    f32 = jnp.float32
    b_, s_ = q.shape[:2]
    z = f_logit.astype(f32)
    lbf = lb.astype(f32)
    log_f = jnp.logaddexp(jnp.log(lbf), jnp.log1p(-lbf) + jax.nn.log_sigmoid(z))
    k = (1.0 - lbf) * jax.nn.sigmoid(-z)

    def seq_major(t, d):
        return to_chunks(t.astype(f32), HG_HEADS, d).transpose(2, 0, 1, 3, 4)

    xs = (seq_major(q, HG_DK), seq_major(log_f, HG_DK), seq_major(k, HG_DK), seq_major(i_in, HG_DV))
    pos = jnp.arange(CHUNK)
    causal = (pos[:, None] >= pos[None, :])[:, :, None]

    def step(state, inp):
        q_c, lf_c, k_c, v_c = inp
        b = jnp.cumsum(lf_c, axis=2)
        diff = b[:, :, :, None, :] - b[:, :, None, :, :]
        decay = jnp.where(causal, jnp.exp(jnp.where(causal, diff, 0.0)), 0.0)
        scores = jnp.einsum('bhtk,bhjk,bhtjk->bhtj', q_c, k_c, decay)
        o_c = (jnp.einsum('bhtj,bhjv->bhtv', scores, v_c)
               + jnp.einsum('bhtk,bhkv->bhtv', q_c * jnp.exp(b), state))
        b_end = b[:, :, -1:, :]
        state = (jnp.exp(b_end[:, :, 0, :, None]) * state
                 + jnp.einsum('bhjk,bhjv->bhkv', k_c * jnp.exp(b_end - b), v_c))
        return state, o_c

    s0 = jnp.zeros((b_, HG_HEADS, HG_DK, HG_DV), f32)
    _, o = lax.scan(step, s0, xs)
    o = o.transpose(1, 0, 3, 2, 4).reshape(b_, s_, HG_HEADS, HG_DV)
    o = o * lax.rsqrt(jnp.mean(o * o, axis=-1, keepdims=True) + EPS) * o_gain.astype(f32).reshape(HG_HEADS, HG_DV)
    o = o.reshape(b_, s_, D_BRANCH) * jax.nn.silu(g.astype(f32))
    return o.astype(q.dtype)


def segsum(t):
    tc = jnp.cumsum(t, axis=-1)
    idx = jnp.arange(t.shape[-1])
    mask = idx[:, None] >= idx[None, :]
    return jnp.where(mask, tc[..., :, None] - tc[..., None, :], -jnp.inf)


def ssd_mixer(z, xbc, dt_raw, conv_w, conv_b, dt_bias, a_log, d_skip, norm_g):
    f32 = jnp.float32
    b_, s_ = z.shape[:2]
    nc = s_ // CHUNK
    xbc = xbc.astype(f32)
    xpad = jnp.pad(xbc, ((0, 0), (SSD_CONV - 1, 0), (0, 0)))
    conv = conv_b.astype(f32)
    for tap in range(SSD_CONV):
        conv = conv + xpad[:, tap:tap + s_, :] * conv_w[:, tap].astype(f32)
    xbc = jax.nn.silu(conv)
    xs, bm, cm = split_cols(xbc, (SSD_INNER, SSD_GROUPS * SSD_DSTATE, SSD_GROUPS * SSD_DSTATE))
    rep = SSD_HEADS // SSD_GROUPS
    xs = xs.reshape(b_, nc, CHUNK, SSD_HEADS, SSD_HEADDIM)
    bh = jnp.repeat(bm.reshape(b_, nc, CHUNK, SSD_GROUPS, SSD_DSTATE), rep, axis=3)
    ch = jnp.repeat(cm.reshape(b_, nc, CHUNK, SSD_GROUPS, SSD_DSTATE), rep, axis=3)
    dt = jax.nn.softplus(dt_raw.astype(f32) + dt_bias.astype(f32))
    a = -jnp.exp(a_log.astype(f32))
    a_dt = (dt * a).reshape(b_, nc, CHUNK, SSD_HEADS).transpose(0, 3, 1, 2)
    xdt = xs * dt.reshape(b_, nc, CHUNK, SSD_HEADS)[..., None]
    a_cum = jnp.cumsum(a_dt, axis=-1)
    l_mat = jnp.exp(segsum(a_dt))
    y_diag = jnp.einsum('bclhn,bcshn,bhcls,bcshp->bclhp', ch, bh, l_mat, xdt)
    decay_states = jnp.exp(a_cum[..., -1:] - a_cum)
    states = jnp.einsum('bclhn,bhcl,bclhp->bchpn', bh, decay_states, xdt)
    states = jnp.concatenate([jnp.zeros_like(states[:, :1]), states], axis=1)
    decay_chunk = jnp.exp(segsum(jnp.pad(a_cum[..., -1], ((0, 0), (0, 0), (1, 0)))))
    states = jnp.einsum('bhzc,bchpn->bzhpn', decay_chunk, states)[:, :-1]
    y_off = jnp.einsum('bclhn,bchpn,bhcl->bclhp', ch, states, jnp.exp(a_cum))
    y = (y_diag + y_off + d_skip.astype(f32)[:, None] * xs).reshape(b_, s_, SSD_INNER)
    y = y * jax.nn.silu(z.astype(f32))
    y = y.reshape(b_, s_, SSD_GROUPS, SSD_INNER // SSD_GROUPS)
    y = y * lax.rsqrt(jnp.mean(y * y, axis=-1, keepdims=True) + EPS)
    y = y.reshape(b_, s_, SSD_INNER) * norm_g.astype(f32)
    return y.astype(z.dtype)


def s5_combine(c1, c2):
    a1r, a1i, b1r, b1i = c1
    a2r, a2i, b2r, b2i = c2
    return (a2r * a1r - a2i * a1i,
            a2r * a1i + a2i * a1r,
            a2r * b1r - a2i * b1i + b2r,
            a2r * b1i + a2i * b1r + b2i)


def s5_mixer(u, a_re, a_im, b_re, b_im, c_re, c_im, d_skip, log_dt, w_glu):
    f32 = jnp.float32
    b_, s_ = u.shape[:2]
    a_re = a_re.astype(f32)
    a_im = a_im.astype(f32)
    step = jnp.exp(log_dt.astype(f32))[:, None]
    mag = jnp.exp(a_re * step)
    lam_re = mag * jnp.cos(a_im * step)
    lam_im = mag * jnp.sin(a_im * step)
    den = a_re * a_re + a_im * a_im
    num_re = lam_re - 1.0
    coef_re = (num_re * a_re + lam_im * a_im) / den
    coef_im = (lam_im * a_re - num_re * a_im) / den
    b_re = b_re.astype(f32)
    b_im = b_im.astype(f32)
    bb_re = coef_re[..., None] * b_re - coef_im[..., None] * b_im
    bb_im = coef_re[..., None] * b_im + coef_im[..., None] * b_re
    uf = u.astype(f32)
    ug = uf.reshape(b_, s_, S5_GROUPS, S5_GROUP_CH)
    bu_re = jnp.einsum('bsgi,gpi->bsgp', ug, bb_re)
    bu_im = jnp.einsum('bsgi,gpi->bsgp', ug, bb_im)
    lam_re_full = jnp.broadcast_to(lam_re, bu_re.shape)
    lam_im_full = jnp.broadcast_to(lam_im, bu_re.shape)
    _, _, h_re, h_im = lax.associative_scan(s5_combine, (lam_re_full, lam_im_full, bu_re, bu_im), axis=1)
    y = (jnp.einsum('bsgp,gip->bsgi', h_re, c_re.astype(f32))
         - jnp.einsum('bsgp,gip->bsgi', h_im, c_im.astype(f32)))
    y = y.reshape(b_, s_, D_BRANCH) + d_skip.astype(f32) * uf
    y = jax.nn.gelu(y)
    y = y * jax.nn.sigmoid(y @ w_glu.astype(f32))
    return y.astype(u.dtype)


def chunk_rel_attention(q, k, v, q_gain, k_gain, rel_bias):
    f32 = jnp.float32
    b_, s_ = q.shape[:2]
    nc = s_ // CHUNK
    q = rmsnorm(q.reshape(b_, s_, ATT_HEADS, ATT_HEADDIM), q_gain).reshape(b_, s_, D_BRANCH)
    k = rmsnorm(k.reshape(b_, s_, ATT_HEADS, ATT_HEADDIM), k_gain).reshape(b_, s_, D_BRANCH)
    qc = to_chunks(q, ATT_HEADS, ATT_HEADDIM)
    kc = to_chunks(k, ATT_HEADS, ATT_HEADDIM)
    vc = to_chunks(v, ATT_HEADS, ATT_HEADDIM)
    pad = ((0, 0), (0, 0), (ATT_LEFT_CHUNKS, 0), (0, 0), (0, 0))
    kp = jnp.pad(kc, pad)
    vp = jnp.pad(vc, pad)
    kband = jnp.concatenate([kp[:, :, j:j + nc] for j in range(ATT_LEFT_CHUNKS + 1)], axis=3)
    vband = jnp.concatenate([vp[:, :, j:j + nc] for j in range(ATT_LEFT_CHUNKS + 1)], axis=3)
    scores = jnp.einsum('bhcqd,bhckd->bhcqk', qc, kband).astype(f32) * (ATT_HEADDIM ** -0.5)
    qpos = jnp.arange(CHUNK)[:, None]
    kpos = jnp.arange(ATT_BAND)[None, :]
    rel = jnp.clip(qpos + ATT_LEFT_CHUNKS * CHUNK - kpos, -ATT_MAX_REL, ATT_MAX_REL) + ATT_MAX_REL
    bias = rel_bias.astype(f32)[:, rel]
    valid = (jnp.arange(nc)[:, None] - ATT_LEFT_CHUNKS + kpos // CHUNK) >= 0
    scores = jnp.where(valid[None, None, :, None, :], scores + bias[None, :, None], NEG_INF)
    probs = jax.nn.softmax(scores, axis=-1).astype(vband.dtype)
    out = jnp.einsum('bhcqk,bhckd->bhcqd', probs, vband)
    return out.transpose(0, 2, 3, 1, 4).reshape(b_, s_, D_BRANCH)


def setup_inputs(seed: int = 0) -> dict:
    key = jax.random.key(seed)
    keys = iter(jax.random.split(key, 40))
    f32 = jnp.float32
    L = DEPTH

    def normal(shape, scale):
        return jax.random.normal(next(keys), shape, f32) * scale

    def uniform(shape, lo, hi):
        return jax.random.uniform(next(keys), shape, f32, lo, hi)

    def gain(shape):
        return 1.0 + normal(shape, 0.02)

    ssd_dt = jnp.exp(uniform((L, SSD_HEADS), math.log(1e-3), math.log(1e-1)))
    s5_a_im = jnp.pi * jnp.arange(S5_STATE, dtype=f32)
    return {
        'x': normal((BATCH, SEQ, D_MODEL), 1.0),
        'p': normal((DEPTH, BATCH, SEQ, D_PLE), 1.0),
        'norm_mix': gain((L, D_MODEL)),
        'w_in': normal((L, D_MODEL, N_IN), D_MODEL ** -0.5),
        'hg_lb_logits': normal((L, HG_HEADS * HG_DK), 0.5),
        'hg_o_norm': gain((L, D_BRANCH)),
        'ssd_conv_w': normal((L, SSD_XBC, SSD_CONV), SSD_CONV ** -0.5),
        'ssd_conv_b': normal((L, SSD_XBC), 0.02),
        'ssd_dt_bias': ssd_dt + jnp.log(-jnp.expm1(-ssd_dt)),
        'ssd_A_log': jnp.log(uniform((L, SSD_HEADS), 1.0, 16.0)),
        'ssd_D': 1.0 + normal((L, SSD_HEADS), 0.1),
        'ssd_norm': gain((L, SSD_INNER)),
        's5_A_re': -0.5 + normal((L, S5_GROUPS, S5_STATE), 0.01),
        's5_A_im': s5_a_im + normal((L, S5_GROUPS, S5_STATE), 0.01),
        's5_B_re': normal((L, S5_GROUPS, S5_STATE, S5_GROUP_CH), (2 * S5_GROUP_CH) ** -0.5),
        's5_B_im': normal((L, S5_GROUPS, S5_STATE, S5_GROUP_CH), (2 * S5_GROUP_CH) ** -0.5),
        's5_C_re': normal((L, S5_GROUPS, S5_GROUP_CH, S5_STATE), (2 * S5_STATE) ** -0.5),
        's5_C_im': normal((L, S5_GROUPS, S5_GROUP_CH, S5_STATE), (2 * S5_STATE) ** -0.5),
        's5_D': normal((L, D_BRANCH), 0.5),
        's5_log_dt': uniform((L, S5_GROUPS), math.log(1e-3), math.log(1e-1)),
        's5_w_glu': normal((L, D_BRANCH, D_BRANCH), D_BRANCH ** -0.5),
        'att_q_norm': gain((L, ATT_HEADDIM)),
        'att_k_norm': gain((L, ATT_HEADDIM)),
        'att_rel_bias': normal((L, ATT_HEADS, 2 * ATT_MAX_REL + 1), 0.1),
        'w_branch': normal((L, N_BRANCH, D_BRANCH, D_MODEL), D_BRANCH ** -0.5),
        'w_out': normal((L, D_MODEL, D_MODEL), D_MODEL ** -0.5),
        'norm_ffn': gain((L, D_MODEL)),
        'w_ff1': normal((L, D_MODEL, D_FF), D_MODEL ** -0.5),
        'w_ff2': normal((L, D_FF, D_MODEL), D_FF ** -0.5),
        'w_ple': normal((L, D_PLE, D_MODEL), D_PLE ** -0.5),
        'norm_ple': gain((L, D_MODEL)),
        'w_ple_gate': normal((L, D_MODEL, D_MODEL), D_MODEL ** -0.5),
    }


def reference(x, p, norm_mix, w_in, hg_lb_logits, hg_o_norm, ssd_conv_w, ssd_conv_b, ssd_dt_bias,
              ssd_A_log, ssd_D, ssd_norm, s5_A_re, s5_A_im, s5_B_re, s5_B_im, s5_C_re, s5_C_im, s5_D,
              s5_log_dt, s5_w_glu, att_q_norm, att_k_norm, att_rel_bias, w_branch, w_out, norm_ffn,
              w_ff1, w_ff2, w_ple, norm_ple, w_ple_gate):
    b_, s_ = x.shape[:2]
    lb_all = jnp.cumsum(jax.nn.softmax(hg_lb_logits.astype(jnp.float32), axis=0), axis=0)
    lb_all = lb_all - lb_all[0:1]
    for i in range(DEPTH):
        h = rmsnorm(x, norm_mix[i])
        proj = h @ w_in[i]
        a_in, b_in, c_in, d_in, gate_in = split_cols(proj, SPLIT_SIZES)

        hq, hf, hi, hg = jnp.split(a_in, 4, axis=-1)
        y_a = hgrn2_mixer(hq, hf, hi, hg, lb_all[i], hg_o_norm[i])

        sz, sxbc, sdt = split_cols(b_in, (SSD_INNER, SSD_XBC, SSD_HEADS))
        y_b = ssd_mixer(sz, sxbc, sdt, ssd_conv_w[i], ssd_conv_b[i], ssd_dt_bias[i], ssd_A_log[i],
                        ssd_D[i], ssd_norm[i])

        y_c = s5_mixer(c_in, s5_A_re[i], s5_A_im[i], s5_B_re[i], s5_B_im[i], s5_C_re[i], s5_C_im[i],
                       s5_D[i], s5_log_dt[i], s5_w_glu[i])

        aq, ak, av = jnp.split(d_in, 3, axis=-1)
        y_d = chunk_rel_attention(aq, ak, av, att_q_norm[i], att_k_norm[i], att_rel_bias[i])

        gates = jax.nn.sigmoid(gate_in.reshape(b_, s_, N_BRANCH, D_MODEL))
        branches = (y_a, y_b, y_c, y_d)
        merged = gates[:, :, 0] * (branches[0] @ w_branch[i, 0])
        for m in range(1, N_BRANCH):
            merged = merged + gates[:, :, m] * (branches[m] @ w_branch[i, m])
        x = x + merged @ w_out[i]

        h2 = rmsnorm(x, norm_ffn[i])
        x = x + jnp.square(jax.nn.relu(h2 @ w_ff1[i])) @ w_ff2[i]

        ple_gate = jax.nn.sigmoid(rmsnorm(x, norm_ple[i]) @ w_ple_gate[i])
        x = x + (p[i] @ w_ple[i]) * ple_gate
    return x
```

```python
import numpy as np
import concourse.bass as bass
import concourse.mybir as mybir
from contextlib import ExitStack

F32 = mybir.dt.float32
BF16 = mybir.dt.bfloat16
I32 = mybir.dt.int32
AF = mybir.ActivationFunctionType
ALU = mybir.AluOpType
AX = mybir.AxisListType

SAME_ENGINE_SYNC = True


class Trk:
    __slots__ = ("w", "r", "name", "excl", "rg")

    def __init__(self, name="", excl=False):
        self.w = None
        self.r = {}
        self.name = name
        self.excl = excl
        self.rg = None


class V:
    __slots__ = ("ap", "trk")

    def __init__(self, ap, trk):
        self.ap = ap
        self.trk = trk

    def __getitem__(self, idx):
        return V(self.ap[idx], self.trk)


class Tile:
    def __init__(self, prog, t, name, nslots=1, excl=False):
        self.t = t
        self.name = name
        self.trks = [Trk(f"{name}.{i}", excl) for i in range(nslots)]

    def __getitem__(self, idx):
        return V(self.t[idx], self.trks)

    def s(self, slot, idx):
        if isinstance(slot, int):
            tr = [self.trks[slot]]
        else:
            tr = [self.trks[i] for i in slot]
        return V(self.t[idx], tr)


class Prog:
    ENG = ("pe", "act", "dve", "pool", "sp")

    def __init__(self):
        self.nc = bass.Bass("TRN2", target_bir_lowering=False)
        nc = self.nc
        self.es = ExitStack()
        self.eng = {"pe": nc.tensor, "act": nc.scalar, "dve": nc.vector, "pool": nc.gpsimd, "sp": nc.sync}
        self.sem = {}
        self.cnt = {}
        self.seen = {e: {} for e in self.ENG}
        for e in self.ENG:
            self.sem[e] = self.es.enter_context(nc.semaphore(f"s_{e}"))
            self.cnt[e] = 0
        self.dma_sems = {}
        self.fence = {}
        self.n_inst = {e: 0 for e in self.ENG}
        self.n_wait = {e: 0 for e in self.ENG}
        self.out_events = []

    def sb(self, name, shape, dtype, nslots=1):
        t = self.es.enter_context(self.nc.sbuf_tensor(name, list(shape), dtype))
        return Tile(self, t, name, nslots)

    def ps(self, name, shape, dtype=F32, nslots=1):
        t = self.es.enter_context(self.nc.psum_tensor(name, list(shape), dtype))
        return Tile(self, t, name, nslots, excl=True)

    def dma_sem(self, key):
        if key not in self.dma_sems:
            s = self.es.enter_context(self.nc.semaphore(f"d_{key}"))
            self.sem[("d", key)] = s
            self.cnt[("d", key)] = 0
            self.dma_sems[key] = ("d", key)
        return self.dma_sems[key]

    def _waits(self, e, reads, writes):
        need = {}

        def add(ev):
            if ev is None:
                return
            k, v = ev
            if need.get(k, 0) < v:
                need[k] = v

        for t in reads:
            add(t.w)
            if t.excl:
                for k, v in t.r.items():
                    if k != e:
                        add((k, v))
        for t in writes:
            add(t.w)
            for k, v in t.r.items():
                add((k, v))
        seen = self.seen[e]
        for k, v in need.items():
            if isinstance(k, tuple):
                v = max(v, self.cnt[k])
                if self.fence.get(k, 0) < v:
                    self.fence[k] = v
            if k == e and (e == "pe" or not SAME_ENGINE_SYNC):
                continue
            if seen.get(k, 0) >= v:
                continue
            self.eng[e].wait_ge(self.sem[k], v)
            self.n_wait[e] += 1
            seen[k] = v

    def _post(self, ev, reads, writes):
        k, v = ev
        for t in writes:
            t.w = ev
            t.r = {}
        for t in reads:
            if t.r.get(k, 0) < v:
                t.r[k] = v

    def op(self, e, fn, reads, writes):
        rt = [t for v in reads if v is not None for t in v.trk]
        wt = [t for v in writes if v is not None for t in v.trk]
        self._waits(e, rt, wt)
        ins = fn(self.eng[e])
        self.cnt[e] += 1
        ins.then_inc(self.sem[e], 1)
        self.n_inst[e] += 1
        ev = (e, self.cnt[e])
        self._post(ev, rt, wt)
        return ev

    def dma(self, q, out, in_, semkey, out_v=None, in_v=None, **kw):
        rt = list(in_v.trk) if in_v is not None else []
        wt = list(out_v.trk) if out_v is not None else []
        self._waits(q, rt, wt)
        k = self.dma_sem(semkey)
        f = self.fence.get(k, 0)
        if f > 0 and self.seen[q].get(k, 0) < f:
            self.eng[q].wait_ge(self.sem[k], f)
            self.seen[q][k] = f
        ins = self.eng[q].dma_start(out=out, in_=in_, **kw)
        self.cnt[k] += 16
        ins.then_inc(self.sem[k], 16)
        self.n_inst[q] += 1
        ev = (k, self.cnt[k])
        self._post(ev, rt, wt)
        return ev

    def wait_event(self, e, ev):
        k, v = ev
        if isinstance(k, tuple):
            v = max(v, self.cnt[k])
            if self.fence.get(k, 0) < v:
                self.fence[k] = v
        if self.seen[e].get(k, 0) < v:
            self.eng[e].wait_ge(self.sem[k], v)
            self.seen[e][k] = v

    def _pe_rg(self, out, lhsT):
        key = (lhsT.ap.base_partition(), lhsT.ap.shape[0])
        for t in out.trk:
            if t.w is not None and t.w[0] == "pe" and t.rg is not None and t.rg != key:
                if self.seen["pe"].get("pe", 0) < t.w[1]:
                    self.eng["pe"].wait_ge(self.sem["pe"], t.w[1])
                    self.seen["pe"]["pe"] = t.w[1]
                    self.n_wait["pe"] += 1
            t.rg = key

    def mm(self, out, lhsT, rhs, start=True, stop=True):
        self._pe_rg(out, lhsT)
        return self.op("pe", lambda g: g.matmul(out.ap, lhsT.ap, rhs.ap, start=start, stop=stop), [lhsT, rhs], [out])

    def tr(self, out, in_, ident):
        self._pe_rg(out, in_)
        return self.op("pe", lambda g: g.transpose(out.ap, in_.ap, ident.ap), [in_, ident], [out])

    def act(self, out, in_, func, bias=None, scale=None, accum=None, e="act"):
        kw = {}
        rd = [in_]
        if bias is not None:
            if isinstance(bias, V):
                kw["bias"] = bias.ap
                rd.append(bias)
            else:
                kw["bias"] = bias
        if scale is not None:
            if isinstance(scale, V):
                kw["scale"] = scale.ap
                rd.append(scale)
            else:
                kw["scale"] = scale
        wr = [out]
        if accum is not None:
            kw["accum_out"] = accum.ap
            wr.append(accum)
        return self.op("act", lambda g: g.activation(out.ap, in_.ap, func, **kw), rd, wr)

    def tt(self, out, a, b, op, e="dve"):
        return self.op(e, lambda g: g.tensor_tensor(out.ap, a.ap, b.ap, op), [a, b], [out])

    def ts(self, out, a, s1, op0, s2=None, op1=None, e="dve", accum=None):
        rd = [a]
        x1 = s1
        if isinstance(s1, V):
            rd.append(s1)
            x1 = s1.ap
        x2 = s2
        if isinstance(s2, V):
            rd.append(s2)
            x2 = s2.ap
        wr = [out]
        kw = {}
        if accum is not None:
            kw["accum_out"] = accum.ap
            wr.append(accum)
        if op1 is None:
            return self.op(e, lambda g: g.tensor_scalar(out.ap, a.ap, x1, None, op0, **kw), rd, wr)
        return self.op(e, lambda g: g.tensor_scalar(out.ap, a.ap, x1, x2, op0, op1, **kw), rd, wr)

    def stt(self, out, a, s, b, op0, op1, e="dve"):
        rd = [a, b]
        x = s
        if isinstance(s, V):
            rd.append(s)
            x = s.ap
        return self.op(e, lambda g: g.scalar_tensor_tensor(out.ap, a.ap, x, b.ap, op0, op1), rd, [out])

    def scan(self, out, d0, d1, init, op0=ALU.mult, op1=ALU.add):
        rd = [d0, d1]
        x = init
        if isinstance(init, V):
            rd.append(init)
            x = init.ap
        return self.op("dve", lambda g: g.tensor_tensor_scan(out.ap, d0.ap, d1.ap, x, op0, op1), rd, [out])

    def copy(self, out, in_, e="dve"):
        if e == "act":
            return self.op("act", lambda g: g.copy(out.ap, in_.ap), [in_], [out])
        return self.op(e, lambda g: g.tensor_copy(out.ap, in_.ap), [in_], [out])

    def memset(self, out, val, e="pool"):
        return self.op(e, lambda g: g.memset(out.ap, val), [], [out])

    def recip(self, out, in_):
        return self.op("dve", lambda g: g.reciprocal(out.ap, in_.ap), [in_], [out])

    def finish(self):
        for ev in self.out_events:
            self.wait_event("sp", ev)
        for k, key in self.dma_sems.items():
            if self.cnt[key] > 0:
                self.wait_event("sp", (key, self.cnt[key]))
        for e in ("pe", "act", "dve", "pool"):
            if self.cnt[e] > 0:
                self.wait_event("sp", (e, self.cnt[e]))
        try:
            self.es.close()
        except AssertionError:
            pass
        return self.nc


from concourse.bass_utils import run_bass_kernel_spmd
import math

NCORES = 8
T = 4096
D = 1024
G = 512
NGRP = T // G
SEQG = 2048 // G
N_IN = 6916
EPS = 1e-6
TWO_PI = 2.0 * math.pi

WNAMES = [
    ("norm_mix", [2, 1024]), ("w_in", [2, 1024, 6916]), ("hg_lb_logits", [2, 256]), ("hg_o_norm", [2, 256]),
    ("ssd_conv_w", [2, 512, 4]), ("ssd_conv_b", [2, 512]), ("ssd_dt_bias", [2, 4]), ("ssd_A_log", [2, 4]),
    ("ssd_D", [2, 4]), ("ssd_norm", [2, 256]), ("s5_A_re", [2, 16, 64]), ("s5_A_im", [2, 16, 64]),
    ("s5_B_re", [2, 16, 64, 16]), ("s5_B_im", [2, 16, 64, 16]), ("s5_C_re", [2, 16, 16, 64]),
    ("s5_C_im", [2, 16, 16, 64]), ("s5_D", [2, 256]), ("s5_log_dt", [2, 16]), ("s5_w_glu", [2, 256, 256]),
    ("att_q_norm", [2, 64]), ("att_k_norm", [2, 64]), ("att_rel_bias", [2, 4, 257]),
    ("w_branch", [2, 4, 256, 1024]), ("w_out", [2, 1024, 1024]), ("norm_ffn", [2, 1024]),
    ("w_ff1", [2, 1024, 4096]), ("w_ff2", [2, 4096, 1024]), ("w_ple", [2, 256, 1024]),
    ("norm_ple", [2, 1024]), ("w_ple_gate", [2, 1024, 1024]),
]


def dap(t, offset, ap):
    return bass.AP(tensor=t.tensor, offset=offset, ap=[list(a) for a in ap])


class StopBuild(Exception):
    pass


def build(n_layers=2, n_groups=NGRP, dbg=False, stop_at=None):
    def chk(name):
        if stop_at == name:
            raise StopBuild()
    P = Prog()
    nc = P.nc
    W = {}
    x_d = nc.dram_tensor("x", [T, D], F32, kind="ExternalInput").ap()
    p_d = nc.dram_tensor("p", [2, T, 256], F32, kind="ExternalInput").ap()
    for n, s in WNAMES:
        W[n] = nc.dram_tensor(n, list(s), F32, kind="ExternalInput").ap()
    y_d = nc.dram_tensor("y", [T, D], F32, kind="ExternalOutput").ap()
    xs1 = nc.dram_tensor("xs1", [T, D], F32, kind="Internal").ap()
    LF = 384
    skA = nc.dram_tensor("skA", [4, LF], F32, kind="Internal").ap()
    skB = nc.dram_tensor("skB", [4, 128, LF + 1], F32, kind="Internal").ap()
    dbg_out = {}

    def dump(name, v, shape):
        if not dbg:
            return
        d = nc.dram_tensor("dbg_" + name, list(shape), v.ap.dtype, kind="ExternalOutput").ap()
        dbg_out[name] = d
        P.out_events.append(P.dma("sp", d, v.ap, "dbgst", in_v=v))

    def barrier():
        for e in P.ENG:
            for e2 in ("pe", "act", "dve", "pool"):
                if e2 != e and P.cnt[e2] > 0:
                    P.wait_event(e, (e2, P.cnt[e2]))

    identf = P.sb("identf", [128, 128], F32)
    onesb = P.sb("onesb", [128, 128], BF16)
    U64 = P.sb("U64", [64, 64], F32)
    SL64 = P.sb("SL64", [64, 64], F32)
    one_c = P.sb("one_c", [128, 1], F32)
    ones256 = P.sb("ones256", [128, 256], F32)
    iof = P.sb("iof", [128, 256], F32)
    ioi = P.sb("ioi", [128, 256], I32)
    P.memset(identf[:], 0.0)
    P.op("pool", lambda g: g.affine_select(identf.t[:], identf.t[:], pattern=[[-1, 128]], compare_op=ALU.not_equal,
                                           fill=1.0, base=0, channel_multiplier=1), [identf[:]], [identf[:]])
    P.memset(onesb[:], 1.0)
    P.memset(U64[:], 1.0)
    P.op("pool", lambda g: g.affine_select(U64.t[:], U64.t[:], pattern=[[1, 64]], compare_op=ALU.is_ge,
                                           fill=0.0, base=0, channel_multiplier=-1), [U64[:]], [U64[:]])
    P.memset(SL64[:], 1.0)
    P.op("pool", lambda g: g.affine_select(SL64.t[:], SL64.t[:], pattern=[[-1, 64]], compare_op=ALU.is_ge,
                                           fill=0.0, base=-1, channel_multiplier=1), [SL64[:]], [SL64[:]])
    P.memset(one_c[:], 1.0)
    eps_c = P.sb("eps_c", [128, 1], F32)
    P.memset(eps_c[:], EPS)
    P.memset(ones256[:], 1.0)
    P.op("pool", lambda g: g.iota(ioi.t[:], pattern=[[1, 256]], base=1, channel_multiplier=0), [], [ioi[:]])
    P.copy(iof[:], ioi[:])

    def U64b(n):
        return V(U64.t[:].unsqueeze(1).broadcast_to([64, n, 64]), U64.trks)

    def SL64b(n):
        return V(SL64.t[:].unsqueeze(1).broadcast_to([64, n, 64]), SL64.trks)

    xT = P.sb("xT", [128, 8, G], F32)
    hT = P.sb("hT", [128, 8, G], BF16)
    rstd = P.sb("rstd", [128, G], F32)
    sqt = P.sb("sqt", [128, 2, G], BF16, nslots=2)
    NB = 3
    wp = P.sb("wpan", [128, NB, 4096], BF16, nslots=NB)
    wp_i = [0]
    pb = [P.ps(f"pb{i}", [128, 512], F32) for i in range(8)]
    pb_i = [0]

    held = set()

    def nb(hold=False):
        while True:
            i_ = pb_i[0] % 8
            pb_i[0] += 1
            if i_ not in held:
                break
        if hold:
            held.add(i_)
        return pb[i_]

    def release(*banks):
        for b in banks:
            held.discard(pb.index(b))

    def panel(src_ap, shape):
        s = wp_i[0] % NB
        wp_i[0] += 1
        n = 1
        for d_ in shape[1:]:
            n *= d_
        assert n <= 4096
        flat = wp.t[:, s, 0:n]
        if len(shape) == 3:
            dst = flat.rearrange("p (a b) -> p a b", a=shape[1])
        elif len(shape) == 4:
            dst = flat.rearrange("p (a b c) -> p a b c", a=shape[1], b=shape[2])
        else:
            dst = flat
        ov = wp.s(s, (slice(None), s, slice(0, n)))
        if isinstance(src_ap, list):
            for sel, sap in src_ap:
                P.dma("pool", dst[sel], sap, f"wp{s}", out_v=ov)
        else:
            P.dma("pool", dst, src_ap, f"wp{s}", out_v=ov)
        return dst, wp.trks[s:s + 1]

    def wpanel_rows(wd, l, r0, nk, c0, ncols):
        rowlen = wd.shape[-1]
        base = l * wd.shape[-2] * rowlen + r0 * rowlen + c0
        src = dap(wd, base, [[rowlen, 128], [128 * rowlen, nk], [1, ncols]])
        return panel(src, [128, nk, ncols])

    yT = [P.sb(f"yT{m}", [128, 2, G], BF16) for m in range(4)]

    gcol = P.sb("gcol", [128, 3, 8], F32)
    ogain = P.sb("ogain", [128, 2], F32)
    ssdng = P.sb("ssdng", [128, 2], F32)
    s5d = P.sb("s5d", [128, 2], F32)
    convw = P.sb("convw", [128, 4, 4], F32)
    convb = P.sb("convb", [128, 4], F32)
    lbcol = P.sb("lbcol", [128, 2], F32)
    omlcol = P.sb("omlcol", [128, 2], F32)
    nomlcol = P.sb("nomlcol", [128, 2], F32)
    lb_bc = P.sb("lb_bc", [64, 256], F32)
    oml_bc = P.sb("oml_bc", [64, 256], F32)
    dtb_bc = P.sb("dtb_bc", [64, 4], F32)
    a_bc = P.sb("a_bc", [64, 4], F32)
    D_bc = P.sb("D_bc", [64, 4], F32)
    gqk = P.sb("gqk", [128, 2], F32)
    biasM = P.sb("biasM", [128, 5, 4, 128], BF16)
    s5_Ere = P.sb("s5_Ere", [128, 8, 256], BF16)
    s5_Eim = P.sb("s5_Eim", [128, 8, 256], BF16)
    s5_Dre = P.sb("s5_Dre", [128, 8, 256], BF16)
    s5_Dim = P.sb("s5_Dim", [128, 8, 256], BF16)
    BBpad = P.sb("BBpad", [128, 8, 2, 128], BF16)
    Cpad = P.sb("Cpad", [128, 8, 2, 128], BF16)
    hgS = P.sb("hgS", [128, 2, 64], F32)
    hgSb = P.sb("hgSb", [128, 2, 64], BF16)
    ssS = P.sb("ssS", [128, 2, 64], F32)
    ssSb = P.sb("ssSb", [128, 2, 64], BF16)
    xc_raw = P.sb("xc_raw", [128, 4, G + 3], F32)
    s5carry = P.sb("s5carry", [128, 8, 2], F32)
    kTh = P.sb("kTh", [128, 2, 1024], BF16)
    Vaug = P.sb("Vaug", [128, 8, 4, 72], BF16)
    P.memset(Vaug[:], 1.0)

    cst_ep = [0]

    def load_small(dst_v, src_ap, q="sp"):
        key = ("d", f"cst{cst_ep[0]}")
        if P.cnt.get(key, 0) > 0 and P.fence.get(key, 0) == P.cnt[key]:
            cst_ep[0] = (cst_ep[0] + 1) % 12
        P.dma(q, dst_v.ap, src_ap, f"cst{cst_ep[0]}", out_v=dst_v, allow_slow_non_contiguous=True)

    def load_cols(dst, ncols, src, base, col_stride, part_stride=1, p0=0, p1=128):
        for c_ in range(ncols):
            load_small(dst[p0:p1, c_:c_ + 1], dap(src, base + c_ * col_stride, [[part_stride, p1 - p0], [1, 1]]))

    def layer_setup(l):
        with ExitStack() as st:
            def tmp(name, shape, dt=F32):
                t = st.enter_context(nc.sbuf_tensor(name + f"_l{l}", list(shape), dt))
                return Tile(P, t, name)
            if True:
                for i, nm in enumerate(["norm_mix", "norm_ffn", "norm_ple"]):
                    load_cols(V(gcol.t[:, i, :], gcol.trks), 8, W[nm], l * 1024, 128)
                load_cols(ogain, 2, W["hg_o_norm"], l * 256, 128)
                load_cols(ssdng, 2, W["ssd_norm"], l * 256, 128)
                load_cols(s5d, 2, W["s5_D"], l * 256, 128)
                load_small(convw[:], dap(W["ssd_conv_w"], l * 2048, [[4, 128], [512, 4], [1, 4]]))
                load_cols(convb, 4, W["ssd_conv_b"], l * 512, 128)
                lg = tmp("lg", [128, 2, 2])
                lgb = tmp("lgb", [64, 2, 256])
                for li in range(2):
                    load_cols(V(lg.t[:, li, :], lg.trks), 2, W["hg_lb_logits"], li * 256, 128)
                    load_small(lgb[:, li, :], dap(W["hg_lb_logits"], li * 256, [[0, 64], [1, 256]]))
                if l == 0:
                    P.memset(lbcol[:], 0.0, e="dve")
                    P.memset(lb_bc[:], 0.0, e="dve")
                else:
                    P.tt(lbcol[:], lg[:, 1, :], lg[:, 0, :], ALU.subtract)
                    P.act(lbcol[:], lbcol[:], AF.Sigmoid)
                    P.tt(lb_bc[:], lgb[:, 1, :], lgb[:, 0, :], ALU.subtract)
                    P.act(lb_bc[:], lb_bc[:], AF.Sigmoid)
                P.ts(omlcol[:], lbcol[:], -1.0, ALU.mult, 1.0, ALU.add)
                P.ts(nomlcol[:], omlcol[:], -1.0, ALU.mult)
                P.ts(oml_bc[:], lb_bc[:], -1.0, ALU.mult, 1.0, ALU.add)
                load_small(dtb_bc[:], dap(W["ssd_dt_bias"], l * 4, [[0, 64], [1, 4]]))
                load_small(a_bc[:], dap(W["ssd_A_log"], l * 4, [[0, 64], [1, 4]]))
                load_small(D_bc[:], dap(W["ssd_D"], l * 4, [[0, 64], [1, 4]]))
                P.act(a_bc[:], a_bc[:], AF.Exp)
                P.ts(a_bc[:], a_bc[:], -1.0, ALU.mult)
                chk('setup_a')
                gqb = tmp("gqb", [128, 64]); gkb = tmp("gkb", [128, 64]); gsum = tmp("gsum", [128, 1])
                load_small(gqb[:], dap(W["att_q_norm"], l * 64, [[0, 128], [1, 64]]))
                load_small(gkb[:], dap(W["att_k_norm"], l * 64, [[0, 128], [1, 64]]))
                rbd = W["att_rel_bias"]
                fpad = tmp("fpad", [4, LF])
                load_small(fpad[:, 0:257], dap(rbd, l * 4 * 257, [[257, 4], [1, 257]]))
                cst4 = tmp("cst4", [128, 4])
                load_cols(cst4, 4, rbd, l * 4 * 257 + 256, 257, part_stride=0)
                are = tmp("are", [128, 8]); aim = tmp("aim", [128, 8]); ldt = tmp("ldt", [128, 8])
                load_cols(are, 8, W["s5_A_re"], l * 1024, 128)
                load_cols(aim, 8, W["s5_A_im"], l * 1024, 128)
                for gl in range(2):
                    load_cols(ldt, 8, W["s5_log_dt"], l * 16 + gl, 2, part_stride=0, p0=gl * 64, p1=(gl + 1) * 64)
                Xbs = [tmp(f"Xb{ri}", [128, 8, 128]) for ri in range(2)]
                Ycs = [tmp(f"Yc{ri}", [32, 8, 128]) for ri in range(2)]
                for ri, (bn, cn) in enumerate([("s5_B_re", "s5_C_re"), ("s5_B_im", "s5_C_im")]):
                    Xb = Xbs[ri]; Yc = Ycs[ri]
                    P.memset(Xb[:], 0.0, e="dve")
                    P.memset(Yc[:], 0.0, e="dve")
                    for gl in range(2):
                        for c4 in range(4):
                            col = 32 * c4 + 16 * gl
                            load_small(Xb[gl * 64:(gl + 1) * 64, c4:8:4, col:col + 16],
                                       dap(W[bn], l * 16384 + (2 * c4 + gl) * 1024, [[16, 64], [8 * 1024, 2], [1, 16]]))
                        load_small(Yc[gl * 16:(gl + 1) * 16, :, gl * 64:(gl + 1) * 64],
                                   dap(W[cn], l * 16384 + gl * 1024, [[64, 16], [2048, 8], [1, 64]]))
                P.tt(gqb[:], gqb[:], gkb[:], ALU.mult)
                for hh in range(2):
                    pr = slice(hh * 64, hh * 64 + 64)
                    P.tt(gqb[pr, :], gqb[pr, :], identf[pr, hh * 64:hh * 64 + 64], ALU.mult)
                P.op("dve", lambda g_: g_.tensor_reduce(gsum.t[:], gqb.t[:], AX.X, ALU.add), [gqb[:]], [gsum[:]])
                P.ts(gqk[:, 0:1], gsum[:], 0.125, ALU.mult)
                P.ts(fpad[:, 257:LF], ones256[0:4, 0:LF - 257], fpad[:, 256:257], ALU.mult)
                e2 = P.dma("sp", skA, fpad.t[:], "sk", in_v=fpad[:])
                P.wait_event("sp", e2)
                e3 = P.dma("sp", dap(skB, 0, [[128 * (LF + 1), 4], [LF + 1, 128], [1, LF]]),
                           dap(skA, 0, [[LF, 4], [0, 128], [1, LF]]), "sk")
                P.wait_event("sp", e3)
                P.wait_event("pool", e3)
                for dl in range(2):
                    load_small(biasM[:, dl, :, :], dap(skB, 128 * dl + 128, [[LF, 128], [128 * (LF + 1), 4], [1, 128]]), q="pool")
                for dl in range(2, 5):
                    for h_ in range(4):
                        P.ts(biasM[:, dl, h_, :], ones256[:, 0:128], cst4[:, h_:h_ + 1], ALU.mult)
                P.memset(biasM[64:128, 0, :, 0:64], -30000.0, e="dve")
                P.memset(biasM[0:64, 4, :, 64:128], -30000.0, e="dve")

                chk('setup_b')
                step = tmp("step", [128, 8]); mag1 = tmp("mag1", [128, 8]); fr = tmp("fr", [128, 8])
                fri = tmp("fri", [128, 8], I32); frf = tmp("frf", [128, 8])
                P.act(step[:], ldt[:], AF.Exp)
                lmag = tmp("lmag", [128, 8])
                P.tt(lmag[:], are[:], step[:], ALU.mult)
                P.act(mag1[:], lmag[:], AF.Exp)
                magp = tmp("magp", [128, 256]); magn = tmp("magn", [128, 256])
                Cph = tmp("Cph", [128, 8, 256]); Sph = tmp("Sph", [128, 8, 256])
                P.tt(fr[:], aim[:], step[:], ALU.mult)
                P.ts(fr[:], fr[:], 1.0 / TWO_PI, ALU.mult)
                P.copy(fri[:], fr[:])
                P.copy(frf[:], fri[:])
                P.tt(fr[:], fr[:], frf[:], ALU.subtract)
                arg = tmp("arg", [128, 256]); argi = tmp("argi", [128, 256], I32); argf = tmp("argf", [128, 256])
                msk = tmp("msk", [128, 256]); r2 = tmp("r2", [128, 256])
                for c in range(8):
                    P.ts(arg[:], iof[:], fr[:, c:c + 1], ALU.mult)
                    P.copy(argi[:], arg[:])
                    P.copy(argf[:], argi[:])
                    P.tt(arg[:], arg[:], argf[:], ALU.subtract)
                    P.ts(msk[:], arg[:], 0.5, ALU.is_gt)
                    P.tt(arg[:], arg[:], msk[:], ALU.subtract)
                    P.ts(msk[:], arg[:], -0.5, ALU.is_lt)
                    P.tt(arg[:], arg[:], msk[:], ALU.add)
                    P.act(Sph[:, c, :], arg[:], AF.Sin, scale=TWO_PI)
                    P.ts(r2[:], arg[:], 0.25, ALU.add)
                    P.ts(msk[:], r2[:], 0.5, ALU.is_gt)
                    P.tt(r2[:], r2[:], msk[:], ALU.subtract)
                    P.act(Cph[:, c, :], r2[:], AF.Sin, scale=TWO_PI)
                lre = tmp("lre", [128, 8]); lim = tmp("lim", [128, 8]); den = tmp("den", [128, 8])
                t1 = tmp("t1", [128, 8]); t2 = tmp("t2", [128, 8]); cre = tmp("cre", [128, 8])
                cim = tmp("cim", [128, 8]); ncre = tmp("ncre", [128, 8])
                P.tt(lre[:], mag1[:], Cph[:, :, 0], ALU.mult)
                P.tt(lim[:], mag1[:], Sph[:, :, 0], ALU.mult)
                P.ts(lre[:], lre[:], -1.0, ALU.add)
                P.tt(den[:], are[:], are[:], ALU.mult)
                P.tt(t1[:], aim[:], aim[:], ALU.mult)
                P.tt(den[:], den[:], t1[:], ALU.add)
                P.recip(den[:], den[:])
                P.tt(t1[:], lre[:], are[:], ALU.mult)
                P.tt(t2[:], lim[:], aim[:], ALU.mult)
                P.tt(t1[:], t1[:], t2[:], ALU.add)
                P.tt(cre[:], t1[:], den[:], ALU.mult)
                P.tt(t1[:], lim[:], are[:], ALU.mult)
                P.tt(t2[:], lre[:], aim[:], ALU.mult)
                P.tt(t1[:], t1[:], t2[:], ALU.subtract)
                P.tt(cim[:], t1[:], den[:], ALU.mult)
                P.ts(ncre[:], cre[:], -1.0, ALU.mult)
                for c in range(8):
                    P.ts(magp[:], iof[:], lmag[:, c:c + 1], ALU.mult)
                    P.act(magn[:], magp[:], AF.Exp, scale=-1.0)
                    P.act(magp[:], magp[:], AF.Exp)
                    P.ts(arg[:], Cph[:, c, :], cre[:, c:c + 1], ALU.mult)
                    P.stt(arg[:], Sph[:, c, :], cim[:, c:c + 1], arg[:], ALU.mult, ALU.add)
                    P.tt(s5_Dre[:, c, :], arg[:], magn[:], ALU.mult)
                    P.ts(arg[:], Cph[:, c, :], cim[:, c:c + 1], ALU.mult)
                    P.stt(arg[:], Sph[:, c, :], ncre[:, c:c + 1], arg[:], ALU.mult, ALU.add)
                    P.tt(s5_Dim[:, c, :], arg[:], magn[:], ALU.mult)
                    P.tt(s5_Ere[:, c, :], Cph[:, c, :], magp[:], ALU.mult)
                    P.tt(s5_Eim[:, c, :], Sph[:, c, :], magp[:], ALU.mult)
                chk('setup_c')
                for ri in range(2):
                    Xb = Xbs[ri]; Yc = Ycs[ri]
                    P.memset(Cpad[:, :, ri, :], 0.0, e="dve")
                    for c in range(8):
                        b_ = nb()
                        P.tr(b_[:, 0:128], Xb[:, c, :], identf[:])
                        P.copy(BBpad[:, c, ri, :], b_[:, 0:128], e="act")
                        b2 = nb()
                        P.tr(b2[:, 0:32], Yc[:, c, :], identf[0:32, 0:32])
                        col = 32 * (c % 4)
                        if ri == 0:
                            P.copy(Cpad[:, c, ri, col:col + 32], b2[:, 0:32], e="act")
                        else:
                            P.ts(Cpad[:, c, ri, col:col + 32], b2[:, 0:32], -1.0, ALU.mult)
                chk('setup_d')
            barrier()

    def norm(gi):
        b_ = nb()
        for c in range(8):
            sq = sqt.s(c % 2, (slice(None), c % 2, slice(None)))
            P.act(sq, xT[:, c, :], AF.Square)
            P.mm(b_[:], onesb[:], sq, start=(c == 0), stop=(c == 7))
        P.act(rstd[:], b_[:], AF.Ln, scale=1.0 / 1024, bias=eps_c[:])
        P.act(rstd[:], rstd[:], AF.Exp, scale=-0.5)
        for c in range(8):
            P.stt(hT[:, c, :], xT[:, c, :], gcol[:, gi, c:c + 1], rstd[:], ALU.mult, ALU.mult)

    def lin_b(bank, pan, ptrk, colsl, rhs_fn, nk, ncols=128):
        for kc in range(nk):
            P.mm(bank[0:ncols, :], V(pan[:, kc, colsl], ptrk), rhs_fn(kc), start=(kc == 0), stop=(kc == nk - 1))

    def lin_a(outv, pan, ptrk, colsl, tok_sl):
        for kc in range(8):
            P.mm(outv, hT[:, kc, tok_sl], V(pan[:, kc, colsl], ptrk), start=(kc == 0), stop=(kc == 7))

    try:
      for l in range(n_layers):
          x_src = x_d if l == 0 else xs1
          x_dst = y_d if l == n_layers - 1 else xs1
          layer_setup(l)
          chk('setup')
          win = W["w_in"]
          for g in range(n_groups):
              seq_start = (g % SEQG == 0)
              tok0 = g * G
              do_dump = dbg and l == dbg_l and g == dbg_g
              sg_st = ExitStack()
              sgT = Tile(P, sg_st.enter_context(nc.sbuf_tensor(f"sgT_{l}_{g}", [128, 16, G], BF16)), "sgT", 1)
              with ExitStack() as gst:
                  def gt(name, shape, dt=F32, nslots=1, _st=gst):
                      t = _st.enter_context(nc.sbuf_tensor(f"{name}_{l}_{g}", list(shape), dt))
                      return Tile(P, t, name, nslots)

                  xst_ = ExitStack()
                  if l == 0:
                      x_tok = Tile(P, xst_.enter_context(nc.sbuf_tensor(f"x_tok_{l}_{g}", [128, 2, 1024], F32)), "x_tok", 2)
                      for i in range(4):
                          s = i % 2
                          P.dma("sp", x_tok.t[:, s, :], x_src[tok0 + i * 128: tok0 + (i + 1) * 128, :], f"xl{s}",
                                out_v=x_tok.s(s, (slice(None), s, slice(None))))
                          for hf in range(2):
                              b_ = nb()
                              for q in range(4):
                                  c = hf * 4 + q
                                  P.tr(b_[:, q * 128:(q + 1) * 128], x_tok.s(s, (slice(None), s, slice(c * 128, (c + 1) * 128))), identf[:])
                              P.copy(xT[:, hf * 4:(hf + 1) * 4, i * 128:(i + 1) * 128],
                                     V(b_.t[:].rearrange("p (a b) -> p a b", a=4), b_.trks), e="act")

                  else:
                      P.dma("sp", xT.t[:], dap(xs1, tok0, [[T, 128], [128 * T, 8], [1, G]]), "xlT", out_v=xT[:])
                  if l == 0:
                      barrier()
                  xst_.close()
                  if seq_start:
                      P.memset(hgS[:], 0.0, e="dve"); P.memset(hgSb[:], 0.0, e="dve")
                      P.memset(ssS[:], 0.0, e="dve"); P.memset(ssSb[:], 0.0, e="dve")
                      P.memset(xc_raw[:, :, 0:3], 0.0, e="dve")
                      P.memset(s5carry[:], 0.0, e="dve")
                  chk('xload')
                  norm(0)
                  chk('norm')
                  if do_dump:
                      dump("hT", hT[:], [128, 8, G])

                  hg_qT = gt("hg_qT", [128, 2, G], BF16); hg_kT = gt("hg_kT", [128, 2, G], BF16); hg_gsT = gt("hg_gsT", [128, 2, G], BF16)
                  hg_lf = gt("hg_lf", [64, 8, 256]); hg_kt = gt("hg_kt", [64, 8, 256], BF16); hg_v = gt("hg_v", [64, 8, 256], BF16)
                  ss_zs = gt("ss_zs", [64, 8, 256], BF16); ss_dt = gt("ss_dt", [64, 8, 4])
                  s5_uTb = gt("s5_uTb", [128, 2, G], BF16)
                  at_qk = gt("at_qk", [128, 4, 512], BF16)
                  tst_ = ExitStack()
                  tA = Tile(P, tst_.enter_context(nc.sbuf_tensor(f"tA_{l}_{g}", [128, 2, G], F32)), "tA", 2)
                  tB = Tile(P, tst_.enter_context(nc.sbuf_tensor(f"tB_{l}_{g}", [128, 2, G], F32)), "tB", 2)

                  pan, ptr = wpanel_rows(win, l, 0, 8, 0, 512)
                  for c2 in range(2):
                      b_ = nb()
                      lin_b(b_, pan, ptr, slice(c2 * 128, (c2 + 1) * 128), lambda kc: hT[:, kc, :], 8)
                      P.copy(hg_qT[:, c2, :], b_[:], e="act")
                  for c2 in range(2):
                      b_ = nb()
                      lin_b(b_, pan, ptr, slice(256 + c2 * 128, 256 + (c2 + 1) * 128), lambda kc: hT[:, kc, :], 8)
                      e_ = tA.s(c2, (slice(None), c2, slice(None)))
                      d_ = tB.s(c2, (slice(None), c2, slice(None)))
                      P.act(e_, b_[:], AF.Exp, scale=-1.0)
                      P.act(d_, e_, AF.Ln, bias=one_c[:])
                      P.act(d_, d_, AF.Exp, scale=-1.0)
                      P.ts(hg_kT[:, c2, :], d_, nomlcol[:, c2:c2 + 1], ALU.mult, omlcol[:, c2:c2 + 1], ALU.add)
                  for ch in range(8):
                      b_ = nb()
                      lin_a(b_[0:64, 0:256], pan, ptr, slice(256, 512), slice(ch * 64, (ch + 1) * 64))
                      s = ch % 2
                      e_ = V(tA.t[0:64, s, 0:256], [tA.trks[s]])
                      d_ = V(tB.t[0:64, s, 0:256], [tB.trks[s]])
                      t_ = V(tB.t[0:64, s, 256:512], [tB.trks[s]])
                      P.act(e_, b_[0:64, 0:256], AF.Exp, scale=-1.0)
                      P.tt(t_, e_, lb_bc[:], ALU.mult)
                      P.act(t_, t_, AF.Ln, bias=one_c[0:64, :])
                      P.act(d_, e_, AF.Ln, bias=one_c[0:64, :])
                      P.tt(hg_lf[:, ch, :], t_, d_, ALU.subtract)
                      P.act(d_, hg_lf[:, ch, :], AF.Exp)
                      P.ts(hg_kt[:, ch, :], d_, -1.0, ALU.mult, 1.0, ALU.add)
                  barrier()
                  tst_.close()
                  chk('pA0')
                  pan, ptr = wpanel_rows(win, l, 0, 8, 512, 512)
                  for ch in range(8):
                      b_ = nb()
                      lin_a(b_[0:64, 0:256], pan, ptr, slice(0, 256), slice(ch * 64, (ch + 1) * 64))
                      P.copy(hg_v[:, ch, :], b_[0:64, 0:256], e="act")
                  for c2 in range(2):
                      b_ = nb()
                      lin_b(b_, pan, ptr, slice(256 + c2 * 128, 256 + (c2 + 1) * 128), lambda kc: hT[:, kc, :], 8)
                      P.act(hg_gsT[:, c2, :], b_[:], AF.Silu)
                  chk('pA1')
                  pan, ptr = wpanel_rows(win, l, 0, 8, 1024, 512)
                  for ch in range(8):
                      b_ = nb()
                      lin_a(b_[0:64, 0:256], pan, ptr, slice(0, 256), slice(ch * 64, (ch + 1) * 64))
                      P.act(ss_zs[:, ch, :], b_[0:64, 0:256], AF.Silu)
                  for c2 in range(2):
                      b_ = nb()
                      lin_b(b_, pan, ptr, slice(256 + c2 * 128, 256 + (c2 + 1) * 128), lambda kc: hT[:, kc, :], 8)
                      P.copy(xc_raw[:, c2, 3:3 + G], b_[:], e="act")
                  chk('pA2')
                  pan, ptr = wpanel_rows(win, l, 0, 8, 1536, 512)
                  for c2 in range(2):
                      b_ = nb()
                      lin_b(b_, pan, ptr, slice(c2 * 128, (c2 + 1) * 128), lambda kc: hT[:, kc, :], 8)
                      P.copy(xc_raw[:, 2 + c2, 3:3 + G], b_[:], e="act")
                  for ch in range(8):
                      b_ = nb()
                      lin_a(b_[0:64, 0:4], pan, ptr, slice(256, 260), slice(ch * 64, (ch + 1) * 64))
                      P.tt(ss_dt[:, ch, :], b_[0:64, 0:4], dtb_bc[:], ALU.add)
                  P.act(ss_dt[:], ss_dt[:], AF.Exp)
                  P.act(ss_dt[:], ss_dt[:], AF.Ln, bias=one_c[0:64, :])
                  chk('pA3')
                  pan, ptr = wpanel_rows(win, l, 0, 8, 1796, 512)
                  for c2 in range(2):
                      b_ = nb()
                      lin_b(b_, pan, ptr, slice(c2 * 128, (c2 + 1) * 128), lambda kc: hT[:, kc, :], 8)
                      P.copy(s5_uTb[:, c2, :], b_[:], e="act")
                  for i in range(4):
                      b_ = nb()
                      lin_a(b_[:, 0:256], pan, ptr, slice(256, 512), slice(i * 128, (i + 1) * 128))
                      P.copy(at_qk[:, i, 0:256], b_[:, 0:256], e="act")
                  chk('pA4')
                  pan, ptr = wpanel_rows(win, l, 0, 8, 2308, 512)
                  for i in range(4):
                      gi_ = (g % SEQG) * 4 + i
                      slot = gi_ % 8
                      b_ = nb()
                      lin_a(b_[:, 0:512], pan, ptr, slice(0, 512), slice(i * 128, (i + 1) * 128))
                      P.copy(at_qk[:, i, 256:512], b_[:, 0:256], e="act")
                      P.copy(Vaug[:, slot, :, 0:64], V(b_.t[:, 256:512].rearrange("p (h d) -> p h d", h=4), b_.trks), e="dve")

                  chk('panels')
                  def hgrn_gen(mt):
                      qtT = mt("qtT", [128, 2, G], BF16); ktT = mt("ktT", [128, 2, G], BF16); qebT = mt("qebT", [128, 2, G], BF16)
                      khat = mt("khat", [64, 8, 256], BF16)
                      mid = mt("mid", [128, 2, 8]); nmid = mt("nmid", [128, 2, 8]); ebend = mt("ebend", [128, 2, 8])
                      E0 = mt("E0", [128, G]); E1 = mt("E1", [128, G]); E2 = mt("E2", [128, G])
                      Er = mt("Er", [64, 2, 256], F32, nslots=2)
                      scs = mt("scs", [64, 2, 256], BF16, nslots=2)
                      osq = mt("osq", [64, 256]); ssum = mt("ssum", [64, 4]); on_ = mt("on", [64, 256])
                      bb = [nb(True), nb(True)]
                      for ch in range(8):
                          for c2 in range(2):
                              P.mm(bb[c2][:, ch * 64:(ch + 1) * 64], hg_lf[:, ch, c2 * 128:(c2 + 1) * 128], U64[:])
                      for ch in range(8):
                          b_ = nb()
                          P.mm(b_[0:64, 0:256], SL64[:], hg_lf[:, ch, :])
                          s = ch % 2
                          er = Er.s(s, (slice(None), s, slice(None)))
                          P.act(er, b_[0:64, 0:256], AF.Exp)
                          P.tt(khat[:, ch, :], hg_kt[:, ch, :], er, ALU.mult)
                      yield
                      for c2 in range(2):
                          P.copy(mid[:, c2, :], V(bb[c2].t[:, 31:512:64], bb[c2].trks))
                          P.ts(nmid[:, c2, :], mid[:, c2, :], -1.0, ALU.mult)
                          P.act(E0[:], bb[c2][:], AF.Exp)
                          P.tt(qebT[:, c2, :], hg_qT[:, c2, :], E0[:], ALU.mult)
                          P.copy(ebend[:, c2, :], E0[:, 63:512:64])
                          for ch in range(8):
                              cs = slice(ch * 64, (ch + 1) * 64)
                              P.act(E1[:, cs], bb[c2][:, cs], AF.Exp, bias=nmid[:, c2, ch:ch + 1])
                              P.act(E2[:, cs], bb[c2][:, cs], AF.Exp, scale=-1.0, bias=mid[:, c2, ch:ch + 1])
                          P.tt(qtT[:, c2, :], hg_qT[:, c2, :], E1[:], ALU.mult)
                          P.tt(ktT[:, c2, :], hg_kT[:, c2, :], E2[:], ALU.mult)
                      yield
                      for ch in range(8):
                          cs = slice(ch * 64, (ch + 1) * 64)
                          s = ch % 2
                          ps_ = nb()
                          for h in range(4):
                              pr = slice((h % 2) * 64, (h % 2) * 64 + 64)
                              P.mm(ps_[0:64, h * 64:(h + 1) * 64], ktT[pr, h // 2, cs], qtT[pr, h // 2, cs])
                          yield
                          sc = scs.s(s, (slice(None), s, slice(None)))
                          P.tt(V(scs.t[:, s, :].rearrange("p (h t) -> p h t", h=4), [scs.trks[s]]),
                               V(ps_.t[0:64, 0:256].rearrange("p (h t) -> p h t", h=4), ps_.trks), U64b(4), ALU.mult)
                          po = nb()
                          for h in range(4):
                              pr = slice((h % 2) * 64, (h % 2) * 64 + 64)
                              hs = slice(h * 64, (h + 1) * 64)
                              P.mm(po[0:64, hs], V(scs.t[:, s, hs], [scs.trks[s]]), hg_v[:, ch, hs], start=True, stop=False)
                              P.mm(po[0:64, hs], qebT[pr, h // 2, cs], hgSb[pr, h // 2, :], start=False, stop=True)
                          yield
                          P.act(osq[:], po[0:64, 0:256], AF.Square)
                          P.op("dve", lambda g_: g_.tensor_reduce(ssum.t[:], osq.t[:].rearrange("p (h d) -> p h d", h=4), AX.X, ALU.add),
                               [osq[:]], [ssum[:]])
                          P.ts(ssum[:], ssum[:], 1.0 / 64, ALU.mult, EPS, ALU.add)
                          P.act(ssum[:], ssum[:], AF.Sqrt)
                          P.recip(ssum[:], ssum[:])
                          P.tt(V(on_.t[:].rearrange("p (h d) -> p h d", h=4), on_.trks),
                               V(po.t[0:64, 0:256].rearrange("p (h d) -> p h d", h=4), po.trks),
                               V(ssum.t[:].unsqueeze(2).broadcast_to([64, 4, 64]), ssum.trks), ALU.mult)
                          pt_ = nb()
                          for c2 in range(2):
                              P.tr(pt_[:, c2 * 64:(c2 + 1) * 64], on_[:, c2 * 128:(c2 + 1) * 128], identf[0:64, 0:64])
                          for c2 in range(2):
                              P.stt(yT[0][:, c2, cs], pt_[:, c2 * 64:(c2 + 1) * 64], ogain[:, c2:c2 + 1], hg_gsT[:, c2, cs], ALU.mult, ALU.mult)
                          yield
                          pS = nb()
                          for c2 in range(2):
                              P.mm(pS[:, c2 * 128:(c2 + 1) * 128], khat[:, ch, c2 * 128:(c2 + 1) * 128], hg_v[:, ch, c2 * 128:(c2 + 1) * 128])
                          for c2 in range(2):
                              for hh in range(2):
                                  pr = slice(hh * 64, hh * 64 + 64)
                                  P.stt(hgS[pr, c2, :], hgS[pr, c2, :], ebend[pr, c2, ch:ch + 1],
                                        pS[pr, c2 * 128 + hh * 64: c2 * 128 + hh * 64 + 64], ALU.mult, ALU.add)
                          P.copy(hgSb[:], hgS[:], e="act")
                      release(*bb)
                      if do_dump:
                          dump("yaT", yT[0][:], [128, 2, G])

                  def ssd_gen(mt):
                      xcs = mt("xcs", [128, 4, G]); xcsb = mt("xcsb", [128, 2, G], BF16)
                      xsB = mt("xsB", [64, 8, 384], BF16)
                      for pc in range(4):
                          P.ts(xcs[:, pc, :], xc_raw[:, pc, 0:G], convw[:, pc, 0:1], ALU.mult, convb[:, pc:pc + 1], ALU.add)
                          for tap in range(1, 4):
                              P.stt(xcs[:, pc, :], xc_raw[:, pc, tap:tap + G], convw[:, pc, tap:tap + 1], xcs[:, pc, :], ALU.mult, ALU.add)
                          P.act(xcs[:, pc, :], xcs[:, pc, :], AF.Silu)
                      P.copy(xc_raw[:, :, 0:3], xc_raw[:, :, G:G + 3])
                      P.copy(xcsb[:], xcs[:, 2:4, :])
                      for ch in range(8):
                          cs = slice(ch * 64, (ch + 1) * 64)
                          b_ = nb()
                          for pc in range(3):
                              P.tr(b_[0:64, pc * 128:(pc + 1) * 128], xcs[:, pc, cs], identf[:])
                          P.copy(xsB[:, ch, :], b_[0:64, 0:384], e="act")
                      yield
                      adt = mt("adt", [64, 4]); A1 = mt("A1", [64, 4, 64]); A3 = mt("A3", [64, 4, 128])
                      Lm = mt("Lm", [64, 256]); Eac = mt("Eac", [128, 256]); dec = mt("dec", [64, 4]); dtd = mt("dtd", [64, 4])
                      MT = mt("MT", [64, 256], BF16); xdt = mt("xdt", [64, 4, 64], BF16); xdtd = mt("xdtd", [64, 4, 64], BF16)
                      Bt = mt("Bt", [64, 128], BF16)
                      CeT = mt("CeT", [128, 2, 64], BF16)
                      yg = mt("yg", [64, 256]); ysq = mt("ysq", [64, 256]); ys2 = mt("ys2", [64, 2]); yn = mt("yn", [64, 256])
                      for ch in range(8):
                          cs = slice(ch * 64, (ch + 1) * 64)
                          xs_v = V(xsB.t[:, ch, 0:256].rearrange("p (h d) -> p h d", h=4), xsB.trks)
                          P.tt(adt[:], ss_dt[:, ch, :], a_bc[:], ALU.mult)
                          P.tt(A1[:], SL64b(4), V(adt.t[:].unsqueeze(2).broadcast_to([64, 4, 64]), adt.trks), ALU.mult)
                          P.copy(A3[:], V(adt.t[:].unsqueeze(2).broadcast_to([64, 4, 128]), adt.trks))
                          pseg = nb(); pac = nb(); prev = nb()
                          for h in range(4):
                              P.mm(pseg[0:64, h * 64:(h + 1) * 64], A1[:, h, :], U64[:])
                          for h in range(4):
                              P.mm(pac[:, h * 64:(h + 1) * 64], A3[:, h, :], U64[:])
                          P.mm(prev[0:64, 0:4], SL64[:], adt[:])
                          yield
                          P.act(Lm[:], pseg[0:64, 0:256], AF.Exp)
                          P.act(Eac[:], pac[:, 0:256], AF.Exp)
                          P.act(dec[:], prev[0:64, 0:4], AF.Exp)
                          pG = nb()
                          for gr in range(2):
                              pr = slice(gr * 64, gr * 64 + 64)
                              P.mm(pG[0:64, gr * 64:(gr + 1) * 64], xcsb[pr, 0, cs], xcsb[pr, 1, cs])
                          P.tt(V(Lm.t[:].rearrange("p (h d) -> p h d", h=4), Lm.trks),
                               V(Lm.t[:].rearrange("p (h d) -> p h d", h=4), Lm.trks), U64b(4), ALU.mult)
                          P.tt(V(MT.t[:].rearrange("p (g h d) -> p g h d", g=2, h=2), MT.trks),
                               V(Lm.t[:].rearrange("p (g h d) -> p g h d", g=2, h=2), Lm.trks),
                               V(pG.t[0:64, 0:128].rearrange("p (g d) -> p g d", g=2).unsqueeze(2).broadcast_to([64, 2, 2, 64]), pG.trks),
                               ALU.mult)
                          P.tt(xdt[:], xs_v, V(ss_dt.t[:, ch, :].unsqueeze(2).broadcast_to([64, 4, 64]), ss_dt.trks), ALU.mult)
                          P.tt(dtd[:], ss_dt[:, ch, :], dec[:], ALU.mult)
                          P.tt(xdtd[:], xs_v, V(dtd.t[:].unsqueeze(2).broadcast_to([64, 4, 64]), dtd.trks), ALU.mult)
                          P.copy(Bt[:], xsB[:, ch, 256:384], e="act")
                          yield
                          for gr in range(2):
                              pr = slice(gr * 64, gr * 64 + 64)
                              P.tt(CeT[pr, :, :],
                                   V(xcs.t[pr, 3, cs].unsqueeze(1).broadcast_to([64, 2, 64]), xcs.trks),
                                   V(Eac.t[pr, gr * 128:(gr + 1) * 128].rearrange("p (h d) -> p h d", h=2), Eac.trks), ALU.mult)
                          py = nb()
                          for h in range(4):
                              gr, hh = h // 2, h % 2
                              pr = slice(gr * 64, gr * 64 + 64)
                              hs = slice(h * 64, (h + 1) * 64)
                              P.mm(py[0:64, hs], MT[:, hs], xdt[:, h, :], start=True, stop=False)
                              P.mm(py[0:64, hs], CeT[pr, hh, :], ssSb[pr, hh, :], start=False, stop=True)
                          yield
                          P.tt(V(yg.t[:].rearrange("p (h d) -> p h d", h=4), yg.trks), xs_v,
                               V(D_bc.t[:].unsqueeze(2).broadcast_to([64, 4, 64]), D_bc.trks), ALU.mult)
                          P.tt(yg[:], yg[:], py[0:64, 0:256], ALU.add)
                          P.tt(yg[:], yg[:], ss_zs[:, ch, :], ALU.mult)
                          P.act(ysq[:], yg[:], AF.Square)
                          P.op("dve", lambda g_: g_.tensor_reduce(ys2.t[:], ysq.t[:].rearrange("p (h d) -> p h d", h=2), AX.X, ALU.add),
                               [ysq[:]], [ys2[:]])
                          P.ts(ys2[:], ys2[:], 1.0 / 128, ALU.mult, EPS, ALU.add)
                          P.act(ys2[:], ys2[:], AF.Sqrt)
                          P.recip(ys2[:], ys2[:])
                          P.tt(V(yn.t[:].rearrange("p (h d) -> p h d", h=2), yn.trks),
                               V(yg.t[:].rearrange("p (h d) -> p h d", h=2), yg.trks),
                               V(ys2.t[:].unsqueeze(2).broadcast_to([64, 2, 128]), ys2.trks), ALU.mult)
                          pt_ = nb()
                          for c2 in range(2):
                              P.tr(pt_[:, c2 * 64:(c2 + 1) * 64], yn[:, c2 * 128:(c2 + 1) * 128], identf[0:64, 0:64])
                          for c2 in range(2):
                              P.ts(yT[1][:, c2, cs], pt_[:, c2 * 64:(c2 + 1) * 64], ssdng[:, c2:c2 + 1], ALU.mult)
                          yield
                          pSt = nb()
                          for h in range(4):
                              P.mm(pSt[:, h * 64:(h + 1) * 64], Bt[:], xdtd[:, h, :])
                          for h in range(4):
                              gr, hh = h // 2, h % 2
                              pr = slice(gr * 64, gr * 64 + 64)
                              P.stt(ssS[pr, hh, :], ssS[pr, hh, :], Eac[pr, h * 64 + 63:h * 64 + 64], pSt[pr, h * 64:(h + 1) * 64], ALU.mult, ALU.add)
                          P.copy(ssSb[:], ssS[:], e="act")
                          yield
                      if do_dump:
                          dump("ybT", yT[1][:], [128, 2, G])

                  def s5_gen(mt):
                      HL = 256
                      wre = mt("wre", [128, HL]); wim = mt("wim", [128, HL]); tt1 = mt("tt1", [128, HL]); tt2 = mt("tt2", [128, HL]); tt3 = mt("tt3", [128, HL]); tt4 = mt("tt4", [128, HL])
                      rre = mt("rre", [128, HL]); rim = mt("rim", [128, HL])
                      hre = mt("hre", [128, 2, HL], BF16, nslots=2); him = mt("him", [128, 2, HL], BF16, nslots=2)
                      y1 = mt("y1", [128, 2, G]); y1b = mt("y1b", [128, 2, G], BF16)
                      yv = mt("yv", [128, HL]); g1 = mt("g1", [128, HL]); sg = mt("sg", [128, G])
                      yield
                      glu_pan, glu_tr = panel(dap(W["s5_w_glu"], l * 65536, [[256, 128], [128 * 256, 2], [1, 256]]), [128, 2, 256])
                      for hf in range(2):
                          ts_ = slice(hf * HL, (hf + 1) * HL)
                          pyc = [nb(True), nb(True)]
                          for c in range(8):
                              pbr = nb(); pbi = nb()
                              P.mm(pbr[:, 0:HL], BBpad[:, c, 0, :], s5_uTb[:, c // 4, ts_])
                              P.mm(pbi[:, 0:HL], BBpad[:, c, 1, :], s5_uTb[:, c // 4, ts_])
                              yield
                              P.tt(wre[:], s5_Dre[:, c, :], pbr[:, 0:HL], ALU.mult)
                              P.tt(tt1[:], s5_Dim[:, c, :], pbi[:, 0:HL], ALU.mult)
                              P.tt(wim[:], s5_Dre[:, c, :], pbi[:, 0:HL], ALU.mult)
                              P.tt(tt2[:], s5_Dim[:, c, :], pbr[:, 0:HL], ALU.mult)
                              P.tt(wre[:], wre[:], tt1[:], ALU.subtract)
                              P.tt(wim[:], wim[:], tt2[:], ALU.add)
                              P.scan(rre[:], ones256[:], wre[:], s5carry[:, c, 0:1])
                              P.scan(rim[:], ones256[:], wim[:], s5carry[:, c, 1:2])
                              yield
                              s = c % 2
                              hr = hre.s(s, (slice(None), s, slice(None))); hi = him.s(s, (slice(None), s, slice(None)))
                              P.tt(tt1[:], s5_Ere[:, c, :], rre[:], ALU.mult)
                              P.tt(tt2[:], s5_Eim[:, c, :], rim[:], ALU.mult)
                              P.tt(tt3[:], s5_Ere[:, c, :], rim[:], ALU.mult)
                              P.tt(tt4[:], s5_Eim[:, c, :], rre[:], ALU.mult)
                              P.tt(hr, tt1[:], tt2[:], ALU.subtract)
                              P.tt(s5carry[:, c, 0:1], tt1[:, HL - 1:HL], tt2[:, HL - 1:HL], ALU.subtract)
                              P.tt(hi, tt3[:], tt4[:], ALU.add)
                              P.tt(s5carry[:, c, 1:2], tt3[:, HL - 1:HL], tt4[:, HL - 1:HL], ALU.add)
                              P.mm(pyc[c // 4][:, 0:HL], Cpad[:, c, 0, :], hr, start=(c % 4 == 0), stop=False)
                              P.mm(pyc[c // 4][:, 0:HL], Cpad[:, c, 1, :], hi, start=False, stop=(c % 4 == 3))
                              yield
                          for cc in range(2):
                              P.stt(yv[:], s5_uTb[:, cc, ts_], s5d[:, cc:cc + 1], pyc[cc][:, 0:HL], ALU.mult, ALU.add)
                              P.tt(g1[:], yv[:], yv[:], ALU.mult)
                              P.ts(g1[:], g1[:], 0.044715, ALU.mult, 1.0, ALU.add)
                              P.tt(g1[:], g1[:], yv[:], ALU.mult)
                              P.act(g1[:], g1[:], AF.Sigmoid, scale=1.5957691216057308)
                              P.tt(y1[:, cc, ts_], yv[:], g1[:], ALU.mult)
                              P.copy(y1b[:, cc, ts_], y1[:, cc, ts_], e="act")
                          release(*pyc)
                          if hf == 0:
                              yield "HALF"
                      for cc in range(2):
                          b_ = nb()
                          for kc in range(2):
                              P.mm(b_[:], V(glu_pan[:, kc, cc * 128:(cc + 1) * 128], glu_tr), y1b[:, kc, :], start=(kc == 0), stop=(kc == 1))
                          P.act(sg[:], b_[:], AF.Sigmoid)
                          P.tt(yT[2][:, cc, :], y1[:, cc, :], sg[:], ALU.mult)
                      if do_dump:
                          dump("ycT", yT[2][:], [128, 2, G])

                  def att_gen(mt):
                      qsq = mt("qsq", [128, 512]); qss = mt("qss", [128, 8]); qkn = mt("qkn", [128, 512])
                      qT = mt("qT", [128, 2, G], BF16)
                      pT = mt("pT", [128, 5, 512], BF16, nslots=5)
                      sS = mt("sS", [128, 1, 512], F32, nslots=1)
                      rs = mt("rs", [128, 4]); on_ = mt("aon", [128, 256])
                      for i in range(4):
                          gi_ = (g % SEQG) * 4 + i
                          slot = gi_ % 8
                          ts_ = slice(i * 128, (i + 1) * 128)
                          P.act(qsq[:], at_qk[:, i, :], AF.Square)
                          P.op("dve", lambda g_: g_.tensor_reduce(qss.t[:], qsq.t[:].rearrange("p (h d) -> p h d", h=8), AX.X, ALU.add),
                               [qsq[:]], [qss[:]])
                          P.ts(qss[:], qss[:], 1.0 / 64, ALU.mult, EPS, ALU.add)
                          P.act(qss[:], qss[:], AF.Sqrt)
                          P.recip(qss[:], qss[:])
                          P.tt(V(qkn.t[:].rearrange("p (h d) -> p h d", h=8), qkn.trks),
                               V(at_qk.t[:, i, :].rearrange("p (h d) -> p h d", h=8), at_qk.trks),
                               V(qss.t[:].unsqueeze(2).broadcast_to([128, 8, 64]), qss.trks), ALU.mult)
                          pt_ = nb()
                          for blk in range(4):
                              P.tr(pt_[:, blk * 128:(blk + 1) * 128], qkn[:, blk * 128:(blk + 1) * 128], identf[:])
                          for c2 in range(2):
                              P.ts(qT[:, c2, ts_], pt_[:, c2 * 128:(c2 + 1) * 128], gqk[:, 0:1], ALU.mult)
                              P.copy(kTh[:, c2, slot * 128:(slot + 1) * 128], pt_[:, (2 + c2) * 128:(3 + c2) * 128], e="act")
                          yield
                          kts = [kt for kt in range(gi_ - 4, gi_ + 1) if kt >= 0]
                          for j, kt in enumerate(kts):
                              dl = gi_ - kt
                              ks = kt % 8
                              ps_ = nb()
                              for h in range(4):
                                  pr = slice((h % 2) * 64, (h % 2) * 64 + 64)
                                  P.mm(ps_[:, h * 128:(h + 1) * 128], kTh[pr, h // 2, ks * 128:(ks + 1) * 128], qT[pr, h // 2, ts_])
                              sv = sS.s(0, (slice(None), 0, slice(None)))
                              P.tt(sv, ps_[:], V(biasM.t[:, dl, :, :].rearrange("p h q -> p (h q)"), biasM.trks), ALU.add)
                              P.act(pT.s(j, (slice(None), j, slice(None))), sv, AF.Exp)
                              yield
                          po = nb()
                          for h in range(4):
                              for j, kt in enumerate(kts):
                                  ks = kt % 8
                                  P.mm(po[:, h * 65:(h + 1) * 65], V(pT.t[:, j, h * 128:(h + 1) * 128], [pT.trks[j]]), Vaug[:, ks, h, 0:65],
                                       start=(j == 0), stop=(j == len(kts) - 1))
                          yield
                          P.recip(rs[:], V(po.t[:, 64:260:65], po.trks))
                          P.tt(V(on_.t[:].rearrange("p (h d) -> p h d", h=4), on_.trks),
                               V(po.t[:, 0:260].rearrange("p (h d) -> p h d", h=4)[:, :, 0:64], po.trks),
                               V(rs.t[:].unsqueeze(2).broadcast_to([128, 4, 64]), rs.trks), ALU.mult)
                          pt2 = nb()
                          for c2 in range(2):
                              P.tr(pt2[:, c2 * 128:(c2 + 1) * 128], on_[:, c2 * 128:(c2 + 1) * 128], identf[:])
                          P.copy(yT[3][:, :, ts_], V(pt2.t[:, 0:256].rearrange("p (c t) -> p c t", c=2), pt2.trks), e="act")
                          yield
                      if do_dump:
                          dump("ydT", yT[3][:], [128, 2, G])
                          dump("biasM", biasM[:], [128, 5, 4, 128])
                          dump("qT", qT[:], [128, 2, G])
                          dump("kTh", kTh[:], [128, 2, 1024])
                          dump("Vaug", Vaug[:], [128, 8, 4, 72])
                          dump("pT", pT[:], [128, 5, 512])
                          dump("aon", on_[:], [128, 256])

                  def make_mt(st):
                      def mt(name, shape, dt=F32, nslots=1, _st=st):
                          t = _st.enter_context(nc.sbuf_tensor(f"{name}_{l}_{g}", list(shape), dt))
                          return Tile(P, t, name, nslots)
                      return mt

                  def gates_gen(mt):
                      for m in range(4):
                          gpan, gtr = wpanel_rows(win, l, 0, 8, 2820 + m * 1024, 512)
                          for jj in range(4):
                              pg = nb()
                              for kc in range(8):
                                  P.mm(pg[:], V(gpan[:, kc, jj * 128:(jj + 1) * 128], gtr), hT[:, kc, :], start=(kc == 0), stop=(kc == 7))
                              P.act(sgT[:, m * 4 + jj, :], pg[:], AF.Sigmoid)
                              yield

                  def run_group(gens):
                      with ExitStack() as mst:
                          mt = make_mt(mst)
                          alive = [g_(mt) for g_ in gens]
                          while alive:
                              for g_ in list(alive):
                                  try:
                                      next(g_)
                                  except StopIteration:
                                      alive.remove(g_)
                          barrier()
                  run_group([hgrn_gen, s5_gen])
                  run_group([ssd_gen, att_gen, gates_gen])

              chk('att')
              with ExitStack() as gst:
                  def gt(name, shape, dt=F32, nslots=1, _st=gst):
                      t = _st.enter_context(nc.sbuf_tensor(f"{name}_{l}_{g}", list(shape), dt))
                      return Tile(P, t, name, nslots)
                  sig = gt("sig", [128, 2, G], F32, nslots=2)
                  mergedT = gt("mergedT", [128, 8, G], BF16)
                  acc4 = gt("acc4", [128, 4, G], F32, nslots=4)
                  tmpm = gt("tmpm", [128, 2, G], F32, nslots=2)
                  wbr = gt("wbr", [128, 4, 2, 1024], BF16)
                  wbd = W["w_branch"]
                  for m_ in range(4):
                      P.dma("pool", wbr.t[:, m_, :, :], dap(wbd, l * 4 * 256 * 1024 + m_ * 256 * 1024, [[1024, 128], [128 * 1024, 2], [1, 1024]]),
                            "wbr", out_v=wbr[:])
                  cnt_ = 0
                  for jq in range(2):
                      for m in range(4):
                          if jq == 1:
                              gpan, gtr = wpanel_rows(win, l, 0, 8, 2820 + m * 1024 + jq * 512, 512)
                          for jj in range(4):
                              j = jq * 4 + jj
                              pbm = nb()
                              if jq == 1:
                                  pg = nb()
                                  for kc in range(8):
                                      P.mm(pg[:], V(gpan[:, kc, jj * 128:(jj + 1) * 128], gtr), hT[:, kc, :], start=(kc == 0), stop=(kc == 7))
                              for kc in range(2):
                                  P.mm(pbm[:], wbr[:, m, kc, j * 128:(j + 1) * 128], yT[m][:, kc, :], start=(kc == 0), stop=(kc == 1))
                              s_ = cnt_ % 2
                              cnt_ += 1
                              sv = sig.s(s_, (slice(None), s_, slice(None)))
                              av = acc4.s(jj, (slice(None), jj, slice(None)))
                              tv = tmpm.s(s_, (slice(None), s_, slice(None)))
                              if jq == 1:
                                  P.act(sv, pg[:], AF.Sigmoid)
                              else:
                                  sv = sgT[:, m * 4 + jj, :]
                              if m == 0:
                                  P.tt(av, sv, pbm[:], ALU.mult)
                              else:
                                  P.tt(tv, sv, pbm[:], ALU.mult)
                                  if m < 3:
                                      P.tt(av, av, tv, ALU.add)
                                  else:
                                      P.tt(mergedT[:, j, :], av, tv, ALU.add)
                  if do_dump:
                      dump("mergedT", mergedT[:], [128, 8, G])
                  for pj in range(2):
                      pan, ptr = wpanel_rows(W["w_out"], l, 0, 8, pj * 512, 512)
                      for jj in range(4):
                          b_ = nb()
                          lin_b(b_, pan, ptr, slice(jj * 128, (jj + 1) * 128), lambda kc: mergedT[:, kc, :], 8)
                          P.tt(xT[:, pj * 4 + jj, :], xT[:, pj * 4 + jj, :], b_[:], ALU.add)
                  if do_dump:
                      dump("x1T", xT[:], [128, 8, G])
                  chk('merge')
                  norm(1)
                  aT = gt("aT", [128, 32, G], BF16)
                  rt = gt("rt", [128, 2, G], BF16, nslots=2)
                  for pj in range(8):
                      pan, ptr = wpanel_rows(W["w_ff1"], l, 0, 8, pj * 512, 512)
                      for jj in range(4):
                          b_ = nb()
                          lin_b(b_, pan, ptr, slice(jj * 128, (jj + 1) * 128), lambda kc: hT[:, kc, :], 8)
                          s = jj % 2
                          rv = rt.s(s, (slice(None), s, slice(None)))
                          P.act(rv, b_[:], AF.Relu)
                          P.tt(aT[:, pj * 4 + jj, :], rv, rv, ALU.mult)
                  for hf in range(2):
                      accb = [nb(True) for _ in range(4)]
                      for kq in range(4):
                          pan, ptr = wpanel_rows(W["w_ff2"], l, kq * 1024, 8, hf * 512, 512)
                          for jj in range(4):
                              for kc in range(8):
                                  P.mm(accb[jj][:], V(pan[:, kc, jj * 128:(jj + 1) * 128], ptr), aT[:, kq * 8 + kc, :],
                                       start=(kq == 0 and kc == 0), stop=(kq == 3 and kc == 7))
                      for jj in range(4):
                          P.tt(xT[:, hf * 4 + jj, :], xT[:, hf * 4 + jj, :], accb[jj][:], ALU.add)
                      release(*accb)
                  if do_dump:
                      dump("x2T", xT[:], [128, 8, G])
                  barrier()
              sg_st.close()
              chk('ffn')
              with ExitStack() as gst:
                  def gt(name, shape, dt=F32, nslots=1, _st=gst):
                      t = _st.enter_context(nc.sbuf_tensor(f"{name}_{l}_{g}", list(shape), dt))
                      return Tile(P, t, name, nslots)
                  norm(2)
                  p_tok = gt("p_tok", [128, 2, 256], F32, nslots=2)
                  pT_ = gt("pT_", [128, 2, G], BF16)
                  sgp = gt("sgp", [128, 2, G], F32, nslots=2)
                  x_out = gt("x_out", [128, 2, 1024], F32, nslots=2)
                  for i in range(4):
                      s = i % 2
                      P.dma("sp", p_tok.t[:, s, :], p_d[l, tok0 + i * 128: tok0 + (i + 1) * 128, :], f"pl{s}",
                            out_v=p_tok.s(s, (slice(None), s, slice(None))))
                      b_ = nb()
                      for c2 in range(2):
                          P.tr(b_[:, c2 * 128:(c2 + 1) * 128], p_tok.s(s, (slice(None), s, slice(c2 * 128, (c2 + 1) * 128))), identf[:])
                      P.copy(pT_[:, :, i * 128:(i + 1) * 128], V(b_.t[:, 0:256].rearrange("p (c t) -> p c t", c=2), b_.trks), e="act")
                  plew = gt("plew", [128, 2, 1024], BF16)
                  P.dma("pool", plew.t[:], dap(W["w_ple"], l * 256 * 1024, [[1024, 128], [128 * 1024, 2], [1, 1024]]), "plew", out_v=plew[:])
                  plepan, pletr = plew.t, plew.trks
                  for pj in range(2):
                      pan, ptr = wpanel_rows(W["w_ple_gate"], l, 0, 8, pj * 512, 512)
                      for jj in range(4):
                          j = pj * 4 + jj
                          pg = nb(); pw = nb()
                          lin_b(pg, pan, ptr, slice(jj * 128, (jj + 1) * 128), lambda kc: hT[:, kc, :], 8)
                          for kc in range(2):
                              P.mm(pw[:], V(plepan[:, kc, j * 128:(j + 1) * 128], pletr), pT_[:, kc, :], start=(kc == 0), stop=(kc == 1))
                          s = jj % 2
                          sv = sgp.s(s, (slice(None), s, slice(None)))
                          P.act(sv, pg[:], AF.Sigmoid)
                          P.tt(sv, sv, pw[:], ALU.mult)
                          P.tt(xT[:, j, :], xT[:, j, :], sv, ALU.add)
                  if do_dump:
                      dump("x3T", xT[:], [128, 8, G])
                  if l == n_layers - 1:
                      for i in range(4):
                          s = i % 2
                          for hf in range(2):
                              b_ = nb()
                              for q in range(4):
                                  P.tr(b_[:, q * 128:(q + 1) * 128], xT[:, hf * 4 + q, i * 128:(i + 1) * 128], identf[:])
                              P.copy(V(x_out.t[:, s, hf * 512:(hf + 1) * 512], [x_out.trks[s]]), b_[:], e="act")
                          ev = P.dma("sp", x_dst[tok0 + i * 128: tok0 + (i + 1) * 128, :], x_out.t[:, s, :], f"xst{s}",
                                     in_v=x_out.s(s, (slice(None), s, slice(None))))
                          P.out_events.append(ev)
                      for ev in P.out_events[-4:]:
                          P.wait_event("sp", ev)

                  else:
                      ev = P.dma("sp", dap(xs1, tok0, [[T, 128], [128 * T, 8], [1, G]]), xT.t[:], "xsT", in_v=xT[:])
                      P.wait_event("sp", ev)
                  barrier()

    except StopBuild:
        pass
    P.finish()
    return P, dbg_out


dbg_l = 0
dbg_g = 0
_CACHE = {}


def kernel(**inputs):
    x = np.ascontiguousarray(np.asarray(inputs["x"], dtype=np.float32))
    p = np.ascontiguousarray(np.asarray(inputs["p"], dtype=np.float32))
    if "prog" not in _CACHE:
        _CACHE["prog"] = build()
    P, _ = _CACHE["prog"]
    in_maps = []
    for c in range(NCORES):
        m = {"x": x[2 * c:2 * c + 2].reshape(T, D), "p": p[:, 2 * c:2 * c + 2].reshape(2, T, 256)}
        for n, s in WNAMES:
            m[n] = np.ascontiguousarray(np.asarray(inputs[n], dtype=np.float32))
        in_maps.append(m)
    res = run_bass_kernel_spmd(P.nc, in_maps, core_ids=list(range(NCORES)))
    out = np.concatenate([r["y"].reshape(2, 2048, D) for r in res.results], axis=0)
    return out.astype(np.float32)
```

```python
import numpy as np
import concourse.bass as bass
import concourse.mybir as mybir
from contextlib import ExitStack

F32 = mybir.dt.float32
BF16 = mybir.dt.bfloat16
I32 = mybir.dt.int32
AF = mybir.ActivationFunctionType
ALU = mybir.AluOpType
AX = mybir.AxisListType

SAME_ENGINE_SYNC = True


class Trk:
    __slots__ = ("w", "r", "name", "excl", "rg")

    def __init__(self, name="", excl=False):
        self.w = None
        self.r = {}
        self.name = name
        self.excl = excl
        self.rg = None


class V:
    __slots__ = ("ap", "trk")

    def __init__(self, ap, trk):
        self.ap = ap
        self.trk = trk

    def __getitem__(self, idx):
        return V(self.ap[idx], self.trk)


class Tile:
    def __init__(self, prog, t, name, nslots=1, excl=False):
        self.t = t
        self.name = name
        self.trks = [Trk(f"{name}.{i}", excl) for i in range(nslots)]

    def __getitem__(self, idx):
        return V(self.t[idx], self.trks)

    def s(self, slot, idx):
        if isinstance(slot, int):
            tr = [self.trks[slot]]
        else:
            tr = [self.trks[i] for i in slot]
        return V(self.t[idx], tr)


class Prog:
    ENG = ("pe", "act", "dve", "pool", "sp")

    def __init__(self):
        self.nc = bass.Bass("TRN2", target_bir_lowering=False)
        nc = self.nc
        self.es = ExitStack()
        self.eng = {"pe": nc.tensor, "act": nc.scalar, "dve": nc.vector, "pool": nc.gpsimd, "sp": nc.sync}
        self.sem = {}
        self.cnt = {}
        self.seen = {e: {} for e in self.ENG}
        for e in self.ENG:
            self.sem[e] = self.es.enter_context(nc.semaphore(f"s_{e}"))
            self.cnt[e] = 0
        self.dma_sems = {}
        self.fence = {}
        self.n_inst = {e: 0 for e in self.ENG}
        self.n_wait = {e: 0 for e in self.ENG}
        self.out_events = []

    def sb(self, name, shape, dtype, nslots=1):
        t = self.es.enter_context(self.nc.sbuf_tensor(name, list(shape), dtype))
        return Tile(self, t, name, nslots)

    def ps(self, name, shape, dtype=F32, nslots=1):
        t = self.es.enter_context(self.nc.psum_tensor(name, list(shape), dtype))
        return Tile(self, t, name, nslots, excl=True)

    def dma_sem(self, key):
        if key not in self.dma_sems:
            s = self.es.enter_context(self.nc.semaphore(f"d_{key}"))
            self.sem[("d", key)] = s
            self.cnt[("d", key)] = 0
            self.dma_sems[key] = ("d", key)
        return self.dma_sems[key]

    def _waits(self, e, reads, writes):
        need = {}

        def add(ev):
            if ev is None:
                return
            k, v = ev
            if need.get(k, 0) < v:
                need[k] = v

        for t in reads:
            add(t.w)
            if t.excl:
                for k, v in t.r.items():
                    if k != e:
                        add((k, v))
        for t in writes:
            add(t.w)
            for k, v in t.r.items():
                add((k, v))
        seen = self.seen[e]
        for k, v in need.items():
            if isinstance(k, tuple):
                v = max(v, self.cnt[k])
                if self.fence.get(k, 0) < v:
                    self.fence[k] = v
            if k == e and (e == "pe" or not SAME_ENGINE_SYNC):
                continue
            if seen.get(k, 0) >= v:
                continue
            self.eng[e].wait_ge(self.sem[k], v)
            self.n_wait[e] += 1
            seen[k] = v

    def _post(self, ev, reads, writes):
        k, v = ev
        for t in writes:
            t.w = ev
            t.r = {}
        for t in reads:
            if t.r.get(k, 0) < v:
                t.r[k] = v

    def op(self, e, fn, reads, writes, inc=True):
        rt = [t for v in reads if v is not None for t in v.trk]
        wt = [t for v in writes if v is not None for t in v.trk]
        self._waits(e, rt, wt)
        ins = fn(self.eng[e])
        self.n_inst[e] += 1
        if inc:
            self.cnt[e] += 1
            ins.then_inc(self.sem[e], 1)
            ev = (e, self.cnt[e])
        else:
            ev = (e, self.cnt[e] + 1)
        self._post(ev, rt, wt)
        return ev

    def dma(self, q, out, in_, semkey, out_v=None, in_v=None, **kw):
        rt = list(in_v.trk) if in_v is not None else []
        wt = list(out_v.trk) if out_v is not None else []
        self._waits(q, rt, wt)
        k = self.dma_sem(semkey)
        f = self.fence.get(k, 0)
        if f > 0 and self.seen[q].get(k, 0) < f:
            self.eng[q].wait_ge(self.sem[k], f)
            self.seen[q][k] = f
        ins = self.eng[q].dma_start(out=out, in_=in_, **kw)
        self.cnt[k] += 16
        ins.then_inc(self.sem[k], 16)
        self.n_inst[q] += 1
        ev = (k, self.cnt[k])
        self._post(ev, rt, wt)
        return ev

    def wait_event(self, e, ev):
        k, v = ev
        if isinstance(k, tuple):
            v = max(v, self.cnt[k])
            if self.fence.get(k, 0) < v:
                self.fence[k] = v
        if self.seen[e].get(k, 0) < v:
            self.eng[e].wait_ge(self.sem[k], v)
            self.seen[e][k] = v

    def _pe_rg(self, out, lhsT):
        key = (lhsT.ap.base_partition(), lhsT.ap.shape[0])
        for t in out.trk:
            if t.w is not None and t.w[0] == "pe" and t.rg is not None and t.rg != key:
                if self.seen["pe"].get("pe", 0) < t.w[1]:
                    self.eng["pe"].wait_ge(self.sem["pe"], t.w[1])
                    self.seen["pe"]["pe"] = t.w[1]
                    self.n_wait["pe"] += 1
            t.rg = key

    def mm(self, out, lhsT, rhs, start=True, stop=True, last=True):
        self._pe_rg(out, lhsT)
        return self.op("pe", lambda g: g.matmul(out.ap, lhsT.ap, rhs.ap, start=start, stop=stop), [lhsT, rhs], [out],
                       inc=last)

    def tr(self, out, in_, ident):
        self._pe_rg(out, in_)
        return self.op("pe", lambda g: g.transpose(out.ap, in_.ap, ident.ap), [in_, ident], [out])

    def act(self, out, in_, func, bias=None, scale=None, accum=None, e="act"):
        kw = {}
        rd = [in_]
        if bias is not None:
            if isinstance(bias, V):
                kw["bias"] = bias.ap
                rd.append(bias)
            else:
                kw["bias"] = bias
        if scale is not None:
            if isinstance(scale, V):
                kw["scale"] = scale.ap
                rd.append(scale)
            else:
                kw["scale"] = scale
        wr = [out]
        if accum is not None:
            kw["accum_out"] = accum.ap
            wr.append(accum)
        return self.op("act", lambda g: g.activation(out.ap, in_.ap, func, **kw), rd, wr)

    def tt(self, out, a, b, op, e="dve"):
        return self.op(e, lambda g: g.tensor_tensor(out.ap, a.ap, b.ap, op), [a, b], [out])

    def ts(self, out, a, s1, op0, s2=None, op1=None, e="dve", accum=None):
        rd = [a]
        x1 = s1
        if isinstance(s1, V):
            rd.append(s1)
            x1 = s1.ap
        x2 = s2
        if isinstance(s2, V):
            rd.append(s2)
            x2 = s2.ap
        wr = [out]
        kw = {}
        if accum is not None:
            kw["accum_out"] = accum.ap
            wr.append(accum)
        if op1 is None:
            return self.op(e, lambda g: g.tensor_scalar(out.ap, a.ap, x1, None, op0, **kw), rd, wr)
        return self.op(e, lambda g: g.tensor_scalar(out.ap, a.ap, x1, x2, op0, op1, **kw), rd, wr)

    def stt(self, out, a, s, b, op0, op1, e="dve"):
        rd = [a, b]
        x = s
        if isinstance(s, V):
            rd.append(s)
            x = s.ap
        return self.op(e, lambda g: g.scalar_tensor_tensor(out.ap, a.ap, x, b.ap, op0, op1), rd, [out])

    def scan(self, out, d0, d1, init, op0=ALU.mult, op1=ALU.add):
        rd = [d0, d1]
        x = init
        if isinstance(init, V):
            rd.append(init)
            x = init.ap
        return self.op("dve", lambda g: g.tensor_tensor_scan(out.ap, d0.ap, d1.ap, x, op0, op1), rd, [out])

    def copy(self, out, in_, e="dve"):
        if e == "act":
            return self.op("act", lambda g: g.copy(out.ap, in_.ap), [in_], [out])
        return self.op(e, lambda g: g.tensor_copy(out.ap, in_.ap), [in_], [out])

    def memset(self, out, val, e="pool"):
        return self.op(e, lambda g: g.memset(out.ap, val), [], [out])

    def recip(self, out, in_):
        return self.op("dve", lambda g: g.reciprocal(out.ap, in_.ap), [in_], [out])

    def finish(self):
        for ev in self.out_events:
            self.wait_event("sp", ev)
        for k, key in self.dma_sems.items():
            if self.cnt[key] > 0:
                self.wait_event("sp", (key, self.cnt[key]))
        for e in ("pe", "act", "dve", "pool"):
            if self.cnt[e] > 0:
                self.wait_event("sp", (e, self.cnt[e]))
        try:
            self.es.close()
        except AssertionError:
            pass
        return self.nc


from concourse.bass_utils import run_bass_kernel_spmd
import math

NCORES = 8
T = 4096
D = 1024
G = 512
NGRP = T // G
SEQG = 2048 // G
N_IN = 6916
EPS = 1e-6
TWO_PI = 2.0 * math.pi

WNAMES = [
    ("norm_mix", [2, 1024]), ("w_in", [2, 1024, 6916]), ("hg_lb_logits", [2, 256]), ("hg_o_norm", [2, 256]),
    ("ssd_conv_w", [2, 512, 4]), ("ssd_conv_b", [2, 512]), ("ssd_dt_bias", [2, 4]), ("ssd_A_log", [2, 4]),
    ("ssd_D", [2, 4]), ("ssd_norm", [2, 256]), ("s5_A_re", [2, 16, 64]), ("s5_A_im", [2, 16, 64]),
    ("s5_B_re", [2, 16, 64, 16]), ("s5_B_im", [2, 16, 64, 16]), ("s5_C_re", [2, 16, 16, 64]),
    ("s5_C_im", [2, 16, 16, 64]), ("s5_D", [2, 256]), ("s5_log_dt", [2, 16]), ("s5_w_glu", [2, 256, 256]),
    ("att_q_norm", [2, 64]), ("att_k_norm", [2, 64]), ("att_rel_bias", [2, 4, 257]),
    ("w_branch", [2, 4, 256, 1024]), ("w_out", [2, 1024, 1024]), ("norm_ffn", [2, 1024]),
    ("w_ff1", [2, 1024, 4096]), ("w_ff2", [2, 4096, 1024]), ("w_ple", [2, 256, 1024]),
    ("norm_ple", [2, 1024]), ("w_ple_gate", [2, 1024, 1024]),
]


def dap(t, offset, ap):
    return bass.AP(tensor=t.tensor, offset=offset, ap=[list(a) for a in ap])


class StopBuild(Exception):
    pass


def build(n_layers=2, n_groups=NGRP, dbg=False, stop_at=None):
    def chk(name):
        if stop_at == name:
            raise StopBuild()
    P = Prog()
    nc = P.nc
    W = {}
    x_d = nc.dram_tensor("x", [T, D], F32, kind="ExternalInput").ap()
    p_d = nc.dram_tensor("p", [2, T, 256], F32, kind="ExternalInput").ap()
    for n, s in WNAMES:
        W[n] = nc.dram_tensor(n, list(s), F32, kind="ExternalInput").ap()
    y_d = nc.dram_tensor("y", [T, D], F32, kind="ExternalOutput").ap()
    xs1 = nc.dram_tensor("xs1", [T, D], F32, kind="Internal").ap()
    LF = 384
    skA = nc.dram_tensor("skA", [4, LF], F32, kind="Internal").ap()
    skB = nc.dram_tensor("skB", [4, 128, LF + 1], F32, kind="Internal").ap()
    dbg_out = {}

    def dump(name, v, shape):
        if not dbg:
            return
        d = nc.dram_tensor("dbg_" + name, list(shape), v.ap.dtype, kind="ExternalOutput").ap()
        dbg_out[name] = d
        P.out_events.append(P.dma("sp", d, v.ap, "dbgst", in_v=v))

    def barrier():
        for e in P.ENG:
            for e2 in ("pe", "act", "dve", "pool"):
                if e2 != e and P.cnt[e2] > 0:
                    P.wait_event(e, (e2, P.cnt[e2]))

    identf = P.sb("identf", [128, 128], F32)
    onesb = P.sb("onesb", [128, 128], BF16)
    U64 = P.sb("U64", [64, 64], F32)
    SL64 = P.sb("SL64", [64, 64], F32)
    one_c = P.sb("one_c", [128, 1], F32)
    ones256 = P.sb("ones256", [128, 256], F32)
    iof = P.sb("iof", [128, 256], F32)
    ioi = P.sb("ioi", [128, 256], I32)
    P.memset(identf[:], 0.0)
    P.op("pool", lambda g: g.affine_select(identf.t[:], identf.t[:], pattern=[[-1, 128]], compare_op=ALU.not_equal,
                                           fill=1.0, base=0, channel_multiplier=1), [identf[:]], [identf[:]])
    P.memset(onesb[:], 1.0)
    P.memset(U64[:], 1.0)
    P.op("pool", lambda g: g.affine_select(U64.t[:], U64.t[:], pattern=[[1, 64]], compare_op=ALU.is_ge,
                                           fill=0.0, base=0, channel_multiplier=-1), [U64[:]], [U64[:]])
    P.memset(SL64[:], 1.0)
    P.op("pool", lambda g: g.affine_select(SL64.t[:], SL64.t[:], pattern=[[-1, 64]], compare_op=ALU.is_ge,
                                           fill=0.0, base=-1, channel_multiplier=1), [SL64[:]], [SL64[:]])
    P.memset(one_c[:], 1.0)
    eps_c = P.sb("eps_c", [128, 1], F32)
    P.memset(eps_c[:], EPS)
    P.memset(ones256[:], 1.0)
    P.op("pool", lambda g: g.iota(ioi.t[:], pattern=[[1, 256]], base=1, channel_multiplier=0), [], [ioi[:]])
    P.copy(iof[:], ioi[:])

    def U64b(n):
        return V(U64.t[:].unsqueeze(1).broadcast_to([64, n, 64]), U64.trks)

    def SL64b(n):
        return V(SL64.t[:].unsqueeze(1).broadcast_to([64, n, 64]), SL64.trks)

    xT = P.sb("xT", [128, 8, G], F32)
    hT = P.sb("hT", [128, 8, G], BF16)
    rstd = P.sb("rstd", [128, G], F32)
    sqt = P.sb("sqt", [128, 2, G], BF16, nslots=2)
    NB = 3
    wp = P.sb("wpan", [128, NB, 4096], BF16, nslots=NB)
    wp_i = [0]
    pb = [P.ps(f"pb{i}", [128, 512], F32) for i in range(8)]
    pb_i = [0]

    held = set()

    def nb(hold=False):
        while True:
            i_ = pb_i[0] % 8
            pb_i[0] += 1
            if i_ not in held:
                break
        if hold:
            held.add(i_)
        return pb[i_]

    def release(*banks):
        for b in banks:
            held.discard(pb.index(b))

    def panel(src_ap, shape):
        s = wp_i[0] % NB
        wp_i[0] += 1
        n = 1
        for d_ in shape[1:]:
            n *= d_
        assert n <= 4096
        flat = wp.t[:, s, 0:n]
        if len(shape) == 3:
            dst = flat.rearrange("p (a b) -> p a b", a=shape[1])
        elif len(shape) == 4:
            dst = flat.rearrange("p (a b c) -> p a b c", a=shape[1], b=shape[2])
        else:
            dst = flat
        ov = wp.s(s, (slice(None), s, slice(0, n)))
        if isinstance(src_ap, list):
            for sel, sap in src_ap:
                P.dma("pool", dst[sel], sap, f"wp{s}", out_v=ov)
        else:
            P.dma("pool", dst, src_ap, f"wp{s}", out_v=ov)
        return dst, wp.trks[s:s + 1]

    def wpanel_rows(wd, l, r0, nk, c0, ncols):
        rowlen = wd.shape[-1]
        base = l * wd.shape[-2] * rowlen + r0 * rowlen + c0
        src = dap(wd, base, [[rowlen, 128], [128 * rowlen, nk], [1, ncols]])
        return panel(src, [128, nk, ncols])

    yT = [P.sb(f"yT{m}", [128, 2, G], BF16) for m in range(4)]

    gcol = P.sb("gcol", [128, 3, 8], F32)
    ogain = P.sb("ogain", [128, 2], F32)
    ssdng = P.sb("ssdng", [128, 2], F32)
    s5d = P.sb("s5d", [128, 2], F32)
    convw = P.sb("convw", [128, 4, 4], F32)
    convb = P.sb("convb", [128, 4], F32)
    lbcol = P.sb("lbcol", [128, 2], F32)
    omlcol = P.sb("omlcol", [128, 2], F32)
    nomlcol = P.sb("nomlcol", [128, 2], F32)
    lb_bc = P.sb("lb_bc", [64, 256], F32)
    oml_bc = P.sb("oml_bc", [64, 256], F32)
    dtb_bc = P.sb("dtb_bc", [64, 4], F32)
    a_bc = P.sb("a_bc", [64, 4], F32)
    D_bc = P.sb("D_bc", [64, 4], F32)
    gqk = P.sb("gqk", [128, 2], F32)
    biasM = P.sb("biasM", [128, 5, 4, 128], BF16)
    s5_Ere = P.sb("s5_Ere", [128, 8, 256], BF16)
    s5_Eim = P.sb("s5_Eim", [128, 8, 256], BF16)
    s5_Dre = P.sb("s5_Dre", [128, 8, 256], BF16)
    s5_Dim = P.sb("s5_Dim", [128, 8, 256], BF16)
    BBpad = P.sb("BBpad", [128, 8, 2, 128], BF16)
    Cpad = P.sb("Cpad", [128, 8, 2, 128], BF16)
    hgS = P.sb("hgS", [128, 2, 64], F32)
    hgSb = P.sb("hgSb", [128, 2, 64], BF16)
    ssS = P.sb("ssS", [128, 2, 64], F32)
    ssSb = P.sb("ssSb", [128, 2, 64], BF16)
    xc_raw = P.sb("xc_raw", [128, 4, G + 3], F32)
    s5carry = P.sb("s5carry", [128, 8, 2], F32)
    kTh = P.sb("kTh", [128, 2, 1024], BF16)
    Vaug = P.sb("Vaug", [128, 8, 4, 72], BF16)
    P.memset(Vaug[:], 1.0)

    cst_ep = [0]

    def load_small(dst_v, src_ap, q="sp"):
        key = ("d", f"cst{cst_ep[0]}")
        if P.cnt.get(key, 0) > 0 and P.fence.get(key, 0) == P.cnt[key]:
            cst_ep[0] = (cst_ep[0] + 1) % 12
        P.dma(q, dst_v.ap, src_ap, f"cst{cst_ep[0]}", out_v=dst_v, allow_slow_non_contiguous=True)

    def load_cols(dst, ncols, src, base, col_stride, part_stride=1, p0=0, p1=128):
        for c_ in range(ncols):
            load_small(dst[p0:p1, c_:c_ + 1], dap(src, base + c_ * col_stride, [[part_stride, p1 - p0], [1, 1]]),
                       q=("sp" if c_ % 2 == 0 else "act"))

    def layer_setup(l):
        with ExitStack() as st:
            def tmp(name, shape, dt=F32):
                t = st.enter_context(nc.sbuf_tensor(name + f"_l{l}", list(shape), dt))
                return Tile(P, t, name)
            if True:
                for i, nm in enumerate(["norm_mix", "norm_ffn", "norm_ple"]):
                    load_cols(V(gcol.t[:, i, :], gcol.trks), 8, W[nm], l * 1024, 128)
                load_cols(ogain, 2, W["hg_o_norm"], l * 256, 128)
                load_cols(ssdng, 2, W["ssd_norm"], l * 256, 128)
                load_cols(s5d, 2, W["s5_D"], l * 256, 128)
                load_small(convw[:], dap(W["ssd_conv_w"], l * 2048, [[4, 128], [512, 4], [1, 4]]))
                load_cols(convb, 4, W["ssd_conv_b"], l * 512, 128)
                lg = tmp("lg", [128, 2, 2])
                lgb = tmp("lgb", [64, 2, 256])
                for li in range(2):
                    load_cols(V(lg.t[:, li, :], lg.trks), 2, W["hg_lb_logits"], li * 256, 128)
                    load_small(lgb[:, li, :], dap(W["hg_lb_logits"], li * 256, [[0, 64], [1, 256]]))
                if l == 0:
                    P.memset(lbcol[:], 0.0, e="dve")
                    P.memset(lb_bc[:], 0.0, e="dve")
                else:
                    P.tt(lbcol[:], lg[:, 1, :], lg[:, 0, :], ALU.subtract)
                    P.act(lbcol[:], lbcol[:], AF.Sigmoid)
                    P.tt(lb_bc[:], lgb[:, 1, :], lgb[:, 0, :], ALU.subtract)
                    P.act(lb_bc[:], lb_bc[:], AF.Sigmoid)
                P.ts(omlcol[:], lbcol[:], -1.0, ALU.mult, 1.0, ALU.add)
                P.ts(nomlcol[:], omlcol[:], -1.0, ALU.mult)
                P.ts(oml_bc[:], lb_bc[:], -1.0, ALU.mult, 1.0, ALU.add)
                load_small(dtb_bc[:], dap(W["ssd_dt_bias"], l * 4, [[0, 64], [1, 4]]))
                load_small(a_bc[:], dap(W["ssd_A_log"], l * 4, [[0, 64], [1, 4]]))
                load_small(D_bc[:], dap(W["ssd_D"], l * 4, [[0, 64], [1, 4]]))
                P.act(a_bc[:], a_bc[:], AF.Exp)
                P.ts(a_bc[:], a_bc[:], -1.0, ALU.mult)
                chk('setup_a')
                gqb = tmp("gqb", [128, 64]); gkb = tmp("gkb", [128, 64]); gsum = tmp("gsum", [128, 1])
                load_small(gqb[:], dap(W["att_q_norm"], l * 64, [[0, 128], [1, 64]]))
                load_small(gkb[:], dap(W["att_k_norm"], l * 64, [[0, 128], [1, 64]]))
                rbd = W["att_rel_bias"]
                fpad = tmp("fpad", [4, LF])
                load_small(fpad[:, 0:257], dap(rbd, l * 4 * 257, [[257, 4], [1, 257]]))
                cst4 = tmp("cst4", [128, 4])
                load_cols(cst4, 4, rbd, l * 4 * 257 + 256, 257, part_stride=0)
                are = tmp("are", [128, 8]); aim = tmp("aim", [128, 8]); ldt = tmp("ldt", [128, 8])
                load_cols(are, 8, W["s5_A_re"], l * 1024, 128)
                load_cols(aim, 8, W["s5_A_im"], l * 1024, 128)
                for gl in range(2):
                    load_cols(ldt, 8, W["s5_log_dt"], l * 16 + gl, 2, part_stride=0, p0=gl * 64, p1=(gl + 1) * 64)
                Xbs = [tmp(f"Xb{ri}", [128, 8, 128]) for ri in range(2)]
                Ycs = [tmp(f"Yc{ri}", [32, 8, 128]) for ri in range(2)]
                for ri, (bn, cn) in enumerate([("s5_B_re", "s5_C_re"), ("s5_B_im", "s5_C_im")]):
                    Xb = Xbs[ri]; Yc = Ycs[ri]
                    P.memset(Xb[:], 0.0, e="dve")
                    P.memset(Yc[:], 0.0, e="dve")
                    for gl in range(2):
                        for c4 in range(4):
                            col = 32 * c4 + 16 * gl
                            load_small(Xb[gl * 64:(gl + 1) * 64, c4:8:4, col:col + 16],
                                       dap(W[bn], l * 16384 + (2 * c4 + gl) * 1024, [[16, 64], [8 * 1024, 2], [1, 16]]))
                        load_small(Yc[gl * 16:(gl + 1) * 16, :, gl * 64:(gl + 1) * 64],
                                   dap(W[cn], l * 16384 + gl * 1024, [[64, 16], [2048, 8], [1, 64]]))
                P.tt(gqb[:], gqb[:], gkb[:], ALU.mult)
                for hh in range(2):
                    pr = slice(hh * 64, hh * 64 + 64)
                    P.tt(gqb[pr, :], gqb[pr, :], identf[pr, hh * 64:hh * 64 + 64], ALU.mult)
                P.op("dve", lambda g_: g_.tensor_reduce(gsum.t[:], gqb.t[:], AX.X, ALU.add), [gqb[:]], [gsum[:]])
                P.ts(gqk[:, 0:1], gsum[:], 0.125, ALU.mult)
                P.ts(fpad[:, 257:LF], ones256[0:4, 0:LF - 257], fpad[:, 256:257], ALU.mult)
                e2 = P.dma("sp", skA, fpad.t[:], "sk", in_v=fpad[:])
                P.wait_event("sp", e2)
                e3 = P.dma("sp", dap(skB, 0, [[128 * (LF + 1), 4], [LF + 1, 128], [1, LF]]),
                           dap(skA, 0, [[LF, 4], [0, 128], [1, LF]]), "sk")
                P.wait_event("sp", e3)
                P.wait_event("pool", e3)
                for dl in range(2):
                    load_small(biasM[:, dl, :, :], dap(skB, 128 * dl + 128, [[LF, 128], [128 * (LF + 1), 4], [1, 128]]), q="pool")
                for dl in range(2, 5):
                    for h_ in range(4):
                        P.ts(biasM[:, dl, h_, :], ones256[:, 0:128], cst4[:, h_:h_ + 1], ALU.mult)
                P.memset(biasM[64:128, 0, :, 0:64], -30000.0, e="dve")
                P.memset(biasM[0:64, 4, :, 64:128], -30000.0, e="dve")

                chk('setup_b')
                step = tmp("step", [128, 8]); mag1 = tmp("mag1", [128, 8]); fr = tmp("fr", [128, 8])
                fri = tmp("fri", [128, 8], I32); frf = tmp("frf", [128, 8])
                P.act(step[:], ldt[:], AF.Exp)
                lmag = tmp("lmag", [128, 8])
                P.tt(lmag[:], are[:], step[:], ALU.mult)
                P.act(mag1[:], lmag[:], AF.Exp)
                magp = tmp("magp", [128, 256]); magn = tmp("magn", [128, 256])
                Cph = tmp("Cph", [128, 8, 256]); Sph = tmp("Sph", [128, 8, 256])
                P.tt(fr[:], aim[:], step[:], ALU.mult)
                P.ts(fr[:], fr[:], 1.0 / TWO_PI, ALU.mult)
                P.copy(fri[:], fr[:])
                P.copy(frf[:], fri[:])
                P.tt(fr[:], fr[:], frf[:], ALU.subtract)
                arg = tmp("arg", [128, 256]); argi = tmp("argi", [128, 256], I32); argf = tmp("argf", [128, 256])
                msk = tmp("msk", [128, 256]); r2 = tmp("r2", [128, 256])
                for c in range(8):
                    P.ts(arg[:], iof[:], fr[:, c:c + 1], ALU.mult)
                    P.copy(argi[:], arg[:])
                    P.copy(argf[:], argi[:])
                    P.tt(arg[:], arg[:], argf[:], ALU.subtract)
                    P.ts(msk[:], arg[:], 0.5, ALU.is_gt)
                    P.tt(arg[:], arg[:], msk[:], ALU.subtract)
                    P.ts(msk[:], arg[:], -0.5, ALU.is_lt)
                    P.tt(arg[:], arg[:], msk[:], ALU.add)
                    P.act(Sph[:, c, :], arg[:], AF.Sin, scale=TWO_PI)
                    P.ts(r2[:], arg[:], 0.25, ALU.add)
                    P.ts(msk[:], r2[:], 0.5, ALU.is_gt)
                    P.tt(r2[:], r2[:], msk[:], ALU.subtract)
                    P.act(Cph[:, c, :], r2[:], AF.Sin, scale=TWO_PI)
                lre = tmp("lre", [128, 8]); lim = tmp("lim", [128, 8]); den = tmp("den", [128, 8])
                t1 = tmp("t1", [128, 8]); t2 = tmp("t2", [128, 8]); cre = tmp("cre", [128, 8])
                cim = tmp("cim", [128, 8]); ncre = tmp("ncre", [128, 8])
                P.tt(lre[:], mag1[:], Cph[:, :, 0], ALU.mult)
                P.tt(lim[:], mag1[:], Sph[:, :, 0], ALU.mult)
                P.ts(lre[:], lre[:], -1.0, ALU.add)
                P.tt(den[:], are[:], are[:], ALU.mult)
                P.tt(t1[:], aim[:], aim[:], ALU.mult)
                P.tt(den[:], den[:], t1[:], ALU.add)
                P.recip(den[:], den[:])
                P.tt(t1[:], lre[:], are[:], ALU.mult)
                P.tt(t2[:], lim[:], aim[:], ALU.mult)
                P.tt(t1[:], t1[:], t2[:], ALU.add)
                P.tt(cre[:], t1[:], den[:], ALU.mult)
                P.tt(t1[:], lim[:], are[:], ALU.mult)
                P.tt(t2[:], lre[:], aim[:], ALU.mult)
                P.tt(t1[:], t1[:], t2[:], ALU.subtract)
                P.tt(cim[:], t1[:], den[:], ALU.mult)
                P.ts(ncre[:], cre[:], -1.0, ALU.mult)
                for c in range(8):
                    P.ts(magp[:], iof[:], lmag[:, c:c + 1], ALU.mult)
                    P.act(magn[:], magp[:], AF.Exp, scale=-1.0)
                    P.act(magp[:], magp[:], AF.Exp)
                    P.ts(arg[:], Cph[:, c, :], cre[:, c:c + 1], ALU.mult)
                    P.stt(arg[:], Sph[:, c, :], cim[:, c:c + 1], arg[:], ALU.mult, ALU.add)
                    P.tt(s5_Dre[:, c, :], arg[:], magn[:], ALU.mult)
                    P.ts(arg[:], Cph[:, c, :], cim[:, c:c + 1], ALU.mult)
                    P.stt(arg[:], Sph[:, c, :], ncre[:, c:c + 1], arg[:], ALU.mult, ALU.add)
                    P.tt(s5_Dim[:, c, :], arg[:], magn[:], ALU.mult)
                    P.tt(s5_Ere[:, c, :], Cph[:, c, :], magp[:], ALU.mult)
                    P.tt(s5_Eim[:, c, :], Sph[:, c, :], magp[:], ALU.mult)
                chk('setup_c')
                for ri in range(2):
                    Xb = Xbs[ri]; Yc = Ycs[ri]
                    P.memset(Cpad[:, :, ri, :], 0.0, e="dve")
                    for c in range(8):
                        b_ = nb()
                        P.tr(b_[:, 0:128], Xb[:, c, :], identf[:])
                        P.copy(BBpad[:, c, ri, :], b_[:, 0:128], e="act")
                        b2 = nb()
                        P.tr(b2[:, 0:32], Yc[:, c, :], identf[0:32, 0:32])
                        col = 32 * (c % 4)
                        if ri == 0:
                            P.copy(Cpad[:, c, ri, col:col + 32], b2[:, 0:32], e="act")
                        else:
                            P.ts(Cpad[:, c, ri, col:col + 32], b2[:, 0:32], -1.0, ALU.mult)
                chk('setup_d')
            barrier()

    def norm(gi):
        b_ = nb()
        for c in range(8):
            sq = sqt.s(c % 2, (slice(None), c % 2, slice(None)))
            P.act(sq, xT[:, c, :], AF.Square)
            P.mm(b_[:], onesb[:], sq, start=(c == 0), stop=(c == 7))
        P.act(rstd[:], b_[:], AF.Ln, scale=1.0 / 1024, bias=eps_c[:])
        P.act(rstd[:], rstd[:], AF.Exp, scale=-0.5)
        for c in range(8):
            P.stt(hT[:, c, :], xT[:, c, :], gcol[:, gi, c:c + 1], rstd[:], ALU.mult, ALU.mult)

    def lin_b(bank, pan, ptrk, colsl, rhs_fn, nk, ncols=128):
        for kc in range(nk):
            P.mm(bank[0:ncols, :], V(pan[:, kc, colsl], ptrk), rhs_fn(kc), start=(kc == 0), stop=(kc == nk - 1), last=(kc == nk - 1))

    def lin_a(outv, pan, ptrk, colsl, tok_sl):
        for kc in range(8):
            P.mm(outv, hT[:, kc, tok_sl], V(pan[:, kc, colsl], ptrk), start=(kc == 0), stop=(kc == 7), last=(kc == 7))

    try:
      for l in range(n_layers):
          x_src = x_d if l == 0 else xs1
          x_dst = y_d if l == n_layers - 1 else xs1
          layer_setup(l)
          chk('setup')
          win = W["w_in"]
          for g in range(n_groups):
              seq_start = (g % SEQG == 0)
              tok0 = g * G
              do_dump = dbg and l == dbg_l and g == dbg_g
              sg_st = ExitStack()
              sgT = Tile(P, sg_st.enter_context(nc.sbuf_tensor(f"sgT_{l}_{g}", [128, 16, G], BF16)), "sgT", 1)
              with ExitStack() as gst:
                  def gt(name, shape, dt=F32, nslots=1, _st=gst):
                      t = _st.enter_context(nc.sbuf_tensor(f"{name}_{l}_{g}", list(shape), dt))
                      return Tile(P, t, name, nslots)

                  xst_ = ExitStack()
                  if l == 0:
                      x_tok = Tile(P, xst_.enter_context(nc.sbuf_tensor(f"x_tok_{l}_{g}", [128, 2, 1024], F32)), "x_tok", 2)
                      for i in range(4):
                          s = i % 2
                          P.dma("sp", x_tok.t[:, s, :], x_src[tok0 + i * 128: tok0 + (i + 1) * 128, :], f"xl{s}",
                                out_v=x_tok.s(s, (slice(None), s, slice(None))))
                          for hf in range(2):
                              b_ = nb()
                              for q in range(4):
                                  c = hf * 4 + q
                                  P.tr(b_[:, q * 128:(q + 1) * 128], x_tok.s(s, (slice(None), s, slice(c * 128, (c + 1) * 128))), identf[:])
                              P.copy(xT[:, hf * 4:(hf + 1) * 4, i * 128:(i + 1) * 128],
                                     V(b_.t[:].rearrange("p (a b) -> p a b", a=4), b_.trks), e="act")

                  else:
                      P.dma("sp", xT.t[:], dap(xs1, tok0, [[T, 128], [128 * T, 8], [1, G]]), "xlT", out_v=xT[:])
                  if l == 0:
                      barrier()
                  xst_.close()
                  if seq_start:
                      P.memset(hgS[:], 0.0, e="dve"); P.memset(hgSb[:], 0.0, e="dve")
                      P.memset(ssS[:], 0.0, e="dve"); P.memset(ssSb[:], 0.0, e="dve")
                      P.memset(xc_raw[:, :, 0:3], 0.0, e="dve")
                      P.memset(s5carry[:], 0.0, e="dve")
                  chk('xload')
                  norm(0)
                  chk('norm')
                  if do_dump:
                      dump("hT", hT[:], [128, 8, G])

                  hg_qT = gt("hg_qT", [128, 2, G], BF16); hg_kT = gt("hg_kT", [128, 2, G], BF16); hg_gsT = gt("hg_gsT", [128, 2, G], BF16)
                  hg_lf = gt("hg_lf", [64, 8, 256]); hg_kt = gt("hg_kt", [64, 8, 256], BF16); hg_v = gt("hg_v", [64, 8, 256], BF16)
                  ss_zs = gt("ss_zs", [64, 8, 256], BF16); ss_dt = gt("ss_dt", [64, 8, 4])
                  s5_uTb = gt("s5_uTb", [128, 2, G], BF16)
                  at_qk = gt("at_qk", [128, 4, 512], BF16)
                  tst_ = ExitStack()
                  tA = Tile(P, tst_.enter_context(nc.sbuf_tensor(f"tA_{l}_{g}", [128, 2, G], F32)), "tA", 2)
                  tB = Tile(P, tst_.enter_context(nc.sbuf_tensor(f"tB_{l}_{g}", [128, 2, G], F32)), "tB", 2)

                  pan, ptr = wpanel_rows(win, l, 0, 8, 0, 512)
                  for c2 in range(2):
                      b_ = nb()
                      lin_b(b_, pan, ptr, slice(c2 * 128, (c2 + 1) * 128), lambda kc: hT[:, kc, :], 8)
                      P.copy(hg_qT[:, c2, :], b_[:], e="act")
                  for c2 in range(2):
                      b_ = nb()
                      lin_b(b_, pan, ptr, slice(256 + c2 * 128, 256 + (c2 + 1) * 128), lambda kc: hT[:, kc, :], 8)
                      e_ = tA.s(c2, (slice(None), c2, slice(None)))
                      d_ = tB.s(c2, (slice(None), c2, slice(None)))
                      P.act(e_, b_[:], AF.Exp, scale=-1.0)
                      P.act(d_, e_, AF.Ln, bias=one_c[:])
                      P.act(d_, d_, AF.Exp, scale=-1.0)
                      P.ts(hg_kT[:, c2, :], d_, nomlcol[:, c2:c2 + 1], ALU.mult, omlcol[:, c2:c2 + 1], ALU.add)
                  for ch in range(8):
                      b_ = nb()
                      lin_a(b_[0:64, 0:256], pan, ptr, slice(256, 512), slice(ch * 64, (ch + 1) * 64))
                      s = ch % 2
                      e_ = V(tA.t[0:64, s, 0:256], [tA.trks[s]])
                      d_ = V(tB.t[0:64, s, 0:256], [tB.trks[s]])
                      t_ = V(tB.t[0:64, s, 256:512], [tB.trks[s]])
                      P.act(e_, b_[0:64, 0:256], AF.Exp, scale=-1.0)
                      P.tt(t_, e_, lb_bc[:], ALU.mult)
                      P.act(t_, t_, AF.Ln, bias=one_c[0:64, :])
                      P.act(d_, e_, AF.Ln, bias=one_c[0:64, :])
                      P.tt(hg_lf[:, ch, :], t_, d_, ALU.subtract)
                      P.act(d_, hg_lf[:, ch, :], AF.Exp)
                      P.ts(hg_kt[:, ch, :], d_, -1.0, ALU.mult, 1.0, ALU.add)
                  barrier()
                  tst_.close()
                  chk('pA0')
                  pan, ptr = wpanel_rows(win, l, 0, 8, 512, 512)
                  for ch in range(8):
                      b_ = nb()
                      lin_a(b_[0:64, 0:256], pan, ptr, slice(0, 256), slice(ch * 64, (ch + 1) * 64))
                      P.copy(hg_v[:, ch, :], b_[0:64, 0:256], e="act")
                  for c2 in range(2):
                      b_ = nb()
                      lin_b(b_, pan, ptr, slice(256 + c2 * 128, 256 + (c2 + 1) * 128), lambda kc: hT[:, kc, :], 8)
                      P.act(hg_gsT[:, c2, :], b_[:], AF.Silu)
                  chk('pA1')
                  pan, ptr = wpanel_rows(win, l, 0, 8, 1024, 512)
                  for ch in range(8):
                      b_ = nb()
                      lin_a(b_[0:64, 0:256], pan, ptr, slice(0, 256), slice(ch * 64, (ch + 1) * 64))
                      P.act(ss_zs[:, ch, :], b_[0:64, 0:256], AF.Silu)
                  for c2 in range(2):
                      b_ = nb()
                      lin_b(b_, pan, ptr, slice(256 + c2 * 128, 256 + (c2 + 1) * 128), lambda kc: hT[:, kc, :], 8)
                      P.copy(xc_raw[:, c2, 3:3 + G], b_[:], e="act")
                  chk('pA2')
                  pan, ptr = wpanel_rows(win, l, 0, 8, 1536, 512)
                  for c2 in range(2):
                      b_ = nb()
                      lin_b(b_, pan, ptr, slice(c2 * 128, (c2 + 1) * 128), lambda kc: hT[:, kc, :], 8)
                      P.copy(xc_raw[:, 2 + c2, 3:3 + G], b_[:], e="act")
                  for ch in range(8):
                      b_ = nb()
                      lin_a(b_[0:64, 0:4], pan, ptr, slice(256, 260), slice(ch * 64, (ch + 1) * 64))
                      P.tt(ss_dt[:, ch, :], b_[0:64, 0:4], dtb_bc[:], ALU.add)
                  P.act(ss_dt[:], ss_dt[:], AF.Exp)
                  P.act(ss_dt[:], ss_dt[:], AF.Ln, bias=one_c[0:64, :])
                  chk('pA3')
                  pan, ptr = wpanel_rows(win, l, 0, 8, 1796, 512)
                  for c2 in range(2):
                      b_ = nb()
                      lin_b(b_, pan, ptr, slice(c2 * 128, (c2 + 1) * 128), lambda kc: hT[:, kc, :], 8)
                      P.copy(s5_uTb[:, c2, :], b_[:], e="act")
                  for i in range(4):
                      b_ = nb()
                      lin_a(b_[:, 0:256], pan, ptr, slice(256, 512), slice(i * 128, (i + 1) * 128))
                      P.copy(at_qk[:, i, 0:256], b_[:, 0:256], e="act")
                  chk('pA4')
                  pan, ptr = wpanel_rows(win, l, 0, 8, 2308, 512)
                  for i in range(4):
                      gi_ = (g % SEQG) * 4 + i
                      slot = gi_ % 8
                      b_ = nb()
                      lin_a(b_[:, 0:512], pan, ptr, slice(0, 512), slice(i * 128, (i + 1) * 128))
                      P.copy(at_qk[:, i, 256:512], b_[:, 0:256], e="act")
                      P.copy(Vaug[:, slot, :, 0:64], V(b_.t[:, 256:512].rearrange("p (h d) -> p h d", h=4), b_.trks), e="dve")

                  chk('panels')
                  def hgrn_gen(mt):
                      qtT = mt("qtT", [128, 2, G], BF16); ktT = mt("ktT", [128, 2, G], BF16); qebT = mt("qebT", [128, 2, G], BF16)
                      khat = mt("khat", [64, 8, 256], BF16)
                      mid = mt("mid", [128, 2, 8]); nmid = mt("nmid", [128, 2, 8]); ebend = mt("ebend", [128, 2, 8])
                      E0 = mt("E0", [128, G]); E1 = mt("E1", [128, G]); E2 = mt("E2", [128, G])
                      Er = mt("Er", [64, 2, 256], F32, nslots=2)
                      scs = mt("scs", [64, 2, 256], BF16, nslots=2)
                      osq = mt("osq", [64, 256]); ssum = mt("ssum", [64, 4]); on_ = mt("on", [64, 256])
                      bb = [nb(True), nb(True)]
                      for ch in range(8):
                          for c2 in range(2):
                              P.mm(bb[c2][:, ch * 64:(ch + 1) * 64], hg_lf[:, ch, c2 * 128:(c2 + 1) * 128], U64[:])
                      for ch in range(8):
                          b_ = nb()
                          P.mm(b_[0:64, 0:256], SL64[:], hg_lf[:, ch, :])
                          s = ch % 2
                          er = Er.s(s, (slice(None), s, slice(None)))
                          P.act(er, b_[0:64, 0:256], AF.Exp)
                          P.tt(khat[:, ch, :], hg_kt[:, ch, :], er, ALU.mult)
                      yield
                      for c2 in range(2):
                          P.copy(mid[:, c2, :], V(bb[c2].t[:, 31:512:64], bb[c2].trks))
                          P.ts(nmid[:, c2, :], mid[:, c2, :], -1.0, ALU.mult)
                          P.act(E0[:], bb[c2][:], AF.Exp)
                          P.tt(qebT[:, c2, :], hg_qT[:, c2, :], E0[:], ALU.mult)
                          P.copy(ebend[:, c2, :], E0[:, 63:512:64])
                          for ch in range(8):
                              cs = slice(ch * 64, (ch + 1) * 64)
                              P.act(E1[:, cs], bb[c2][:, cs], AF.Exp, bias=nmid[:, c2, ch:ch + 1])
                              P.act(E2[:, cs], bb[c2][:, cs], AF.Exp, scale=-1.0, bias=mid[:, c2, ch:ch + 1])
                          P.tt(qtT[:, c2, :], hg_qT[:, c2, :], E1[:], ALU.mult)
                          P.tt(ktT[:, c2, :], hg_kT[:, c2, :], E2[:], ALU.mult)
                      yield
                      for ch in range(8):
                          cs = slice(ch * 64, (ch + 1) * 64)
                          s = ch % 2
                          ps_ = nb()
                          for h in range(4):
                              pr = slice((h % 2) * 64, (h % 2) * 64 + 64)
                              P.mm(ps_[0:64, h * 64:(h + 1) * 64], ktT[pr, h // 2, cs], qtT[pr, h // 2, cs])
                          yield
                          sc = scs.s(s, (slice(None), s, slice(None)))
                          P.tt(V(scs.t[:, s, :].rearrange("p (h t) -> p h t", h=4), [scs.trks[s]]),
                               V(ps_.t[0:64, 0:256].rearrange("p (h t) -> p h t", h=4), ps_.trks), U64b(4), ALU.mult)
                          po = nb()
                          for h in range(4):
                              pr = slice((h % 2) * 64, (h % 2) * 64 + 64)
                              hs = slice(h * 64, (h + 1) * 64)
                              P.mm(po[0:64, hs], V(scs.t[:, s, hs], [scs.trks[s]]), hg_v[:, ch, hs], start=True, stop=False)
                              P.mm(po[0:64, hs], qebT[pr, h // 2, cs], hgSb[pr, h // 2, :], start=False, stop=True)
                          yield
                          P.act(osq[:], po[0:64, 0:256], AF.Square)
                          P.op("dve", lambda g_: g_.tensor_reduce(ssum.t[:], osq.t[:].rearrange("p (h d) -> p h d", h=4), AX.X, ALU.add),
                               [osq[:]], [ssum[:]])
                          P.ts(ssum[:], ssum[:], 1.0 / 64, ALU.mult, EPS, ALU.add)
                          P.act(ssum[:], ssum[:], AF.Sqrt)
                          P.recip(ssum[:], ssum[:])
                          P.tt(V(on_.t[:].rearrange("p (h d) -> p h d", h=4), on_.trks),
                               V(po.t[0:64, 0:256].rearrange("p (h d) -> p h d", h=4), po.trks),
                               V(ssum.t[:].unsqueeze(2).broadcast_to([64, 4, 64]), ssum.trks), ALU.mult)
                          pt_ = nb()
                          for c2 in range(2):
                              P.tr(pt_[:, c2 * 64:(c2 + 1) * 64], on_[:, c2 * 128:(c2 + 1) * 128], identf[0:64, 0:64])
                          for c2 in range(2):
                              P.stt(yT[0][:, c2, cs], pt_[:, c2 * 64:(c2 + 1) * 64], ogain[:, c2:c2 + 1], hg_gsT[:, c2, cs], ALU.mult, ALU.mult)
                          yield
                          pS = nb()
                          for c2 in range(2):
                              P.mm(pS[:, c2 * 128:(c2 + 1) * 128], khat[:, ch, c2 * 128:(c2 + 1) * 128], hg_v[:, ch, c2 * 128:(c2 + 1) * 128])
                          for c2 in range(2):
                              for hh in range(2):
                                  pr = slice(hh * 64, hh * 64 + 64)
                                  P.stt(hgS[pr, c2, :], hgS[pr, c2, :], ebend[pr, c2, ch:ch + 1],
                                        pS[pr, c2 * 128 + hh * 64: c2 * 128 + hh * 64 + 64], ALU.mult, ALU.add)
                          P.copy(hgSb[:], hgS[:], e="act")
                      release(*bb)
                      if do_dump:
                          dump("yaT", yT[0][:], [128, 2, G])

                  def ssd_gen(mt):
                      xcs = mt("xcs", [128, 4, G]); xcsb = mt("xcsb", [128, 2, G], BF16)
                      xsB = mt("xsB", [64, 8, 384], BF16)
                      for pc in range(4):
                          P.ts(xcs[:, pc, :], xc_raw[:, pc, 0:G], convw[:, pc, 0:1], ALU.mult, convb[:, pc:pc + 1], ALU.add)
                          for tap in range(1, 4):
                              P.stt(xcs[:, pc, :], xc_raw[:, pc, tap:tap + G], convw[:, pc, tap:tap + 1], xcs[:, pc, :], ALU.mult, ALU.add)
                          P.act(xcs[:, pc, :], xcs[:, pc, :], AF.Silu)
                      P.copy(xc_raw[:, :, 0:3], xc_raw[:, :, G:G + 3])
                      P.copy(xcsb[:], xcs[:, 2:4, :])
                      for ch in range(8):
                          cs = slice(ch * 64, (ch + 1) * 64)
                          b_ = nb()
                          for pc in range(3):
                              P.tr(b_[0:64, pc * 128:(pc + 1) * 128], xcs[:, pc, cs], identf[:])
                          P.copy(xsB[:, ch, :], b_[0:64, 0:384], e="act")
                      yield
                      adt = mt("adt", [64, 4]); A1 = mt("A1", [64, 4, 64]); A3 = mt("A3", [64, 4, 128])
                      Lm = mt("Lm", [64, 256]); Eac = mt("Eac", [128, 256]); dec = mt("dec", [64, 4]); dtd = mt("dtd", [64, 4])
                      MT = mt("MT", [64, 256], BF16); xdt = mt("xdt", [64, 4, 64], BF16); xdtd = mt("xdtd", [64, 4, 64], BF16)
                      Bt = mt("Bt", [64, 128], BF16)
                      CeT = mt("CeT", [128, 2, 64], BF16)
                      yg = mt("yg", [64, 256]); ysq = mt("ysq", [64, 256]); ys2 = mt("ys2", [64, 2]); yn = mt("yn", [64, 256])
                      for ch in range(8):
                          cs = slice(ch * 64, (ch + 1) * 64)
                          xs_v = V(xsB.t[:, ch, 0:256].rearrange("p (h d) -> p h d", h=4), xsB.trks)
                          P.tt(adt[:], ss_dt[:, ch, :], a_bc[:], ALU.mult)
                          P.tt(A1[:], SL64b(4), V(adt.t[:].unsqueeze(2).broadcast_to([64, 4, 64]), adt.trks), ALU.mult)
                          P.copy(A3[:], V(adt.t[:].unsqueeze(2).broadcast_to([64, 4, 128]), adt.trks))
                          pseg = nb(); pac = nb(); prev = nb()
                          for h in range(4):
                              P.mm(pseg[0:64, h * 64:(h + 1) * 64], A1[:, h, :], U64[:])
                          for h in range(4):
                              P.mm(pac[:, h * 64:(h + 1) * 64], A3[:, h, :], U64[:])
                          P.mm(prev[0:64, 0:4], SL64[:], adt[:])
                          yield
                          P.act(Lm[:], pseg[0:64, 0:256], AF.Exp)
                          P.act(Eac[:], pac[:, 0:256], AF.Exp)
                          P.act(dec[:], prev[0:64, 0:4], AF.Exp)
                          pG = nb()
                          for gr in range(2):
                              pr = slice(gr * 64, gr * 64 + 64)
                              P.mm(pG[0:64, gr * 64:(gr + 1) * 64], xcsb[pr, 0, cs], xcsb[pr, 1, cs])
                          P.tt(V(Lm.t[:].rearrange("p (h d) -> p h d", h=4), Lm.trks),
                               V(Lm.t[:].rearrange("p (h d) -> p h d", h=4), Lm.trks), U64b(4), ALU.mult)
                          P.tt(V(MT.t[:].rearrange("p (g h d) -> p g h d", g=2, h=2), MT.trks),
                               V(Lm.t[:].rearrange("p (g h d) -> p g h d", g=2, h=2), Lm.trks),
                               V(pG.t[0:64, 0:128].rearrange("p (g d) -> p g d", g=2).unsqueeze(2).broadcast_to([64, 2, 2, 64]), pG.trks),
                               ALU.mult)
                          P.tt(xdt[:], xs_v, V(ss_dt.t[:, ch, :].unsqueeze(2).broadcast_to([64, 4, 64]), ss_dt.trks), ALU.mult)
                          P.tt(dtd[:], ss_dt[:, ch, :], dec[:], ALU.mult)
                          P.tt(xdtd[:], xs_v, V(dtd.t[:].unsqueeze(2).broadcast_to([64, 4, 64]), dtd.trks), ALU.mult)
                          P.copy(Bt[:], xsB[:, ch, 256:384], e="act")
                          yield
                          for gr in range(2):
                              pr = slice(gr * 64, gr * 64 + 64)
                              P.tt(CeT[pr, :, :],
                                   V(xcs.t[pr, 3, cs].unsqueeze(1).broadcast_to([64, 2, 64]), xcs.trks),
                                   V(Eac.t[pr, gr * 128:(gr + 1) * 128].rearrange("p (h d) -> p h d", h=2), Eac.trks), ALU.mult)
                          py = nb()
                          for h in range(4):
                              gr, hh = h // 2, h % 2
                              pr = slice(gr * 64, gr * 64 + 64)
                              hs = slice(h * 64, (h + 1) * 64)
                              P.mm(py[0:64, hs], MT[:, hs], xdt[:, h, :], start=True, stop=False)
                              P.mm(py[0:64, hs], CeT[pr, hh, :], ssSb[pr, hh, :], start=False, stop=True)
                          yield
                          P.tt(V(yg.t[:].rearrange("p (h d) -> p h d", h=4), yg.trks), xs_v,
                               V(D_bc.t[:].unsqueeze(2).broadcast_to([64, 4, 64]), D_bc.trks), ALU.mult)
                          P.tt(yg[:], yg[:], py[0:64, 0:256], ALU.add)
                          P.tt(yg[:], yg[:], ss_zs[:, ch, :], ALU.mult)
                          P.act(ysq[:], yg[:], AF.Square)
                          P.op("dve", lambda g_: g_.tensor_reduce(ys2.t[:], ysq.t[:].rearrange("p (h d) -> p h d", h=2), AX.X, ALU.add),
                               [ysq[:]], [ys2[:]])
                          P.ts(ys2[:], ys2[:], 1.0 / 128, ALU.mult, EPS, ALU.add)
                          P.act(ys2[:], ys2[:], AF.Sqrt)
                          P.recip(ys2[:], ys2[:])
                          P.tt(V(yn.t[:].rearrange("p (h d) -> p h d", h=2), yn.trks),
                               V(yg.t[:].rearrange("p (h d) -> p h d", h=2), yg.trks),
                               V(ys2.t[:].unsqueeze(2).broadcast_to([64, 2, 128]), ys2.trks), ALU.mult)
                          pt_ = nb()
                          for c2 in range(2):
                              P.tr(pt_[:, c2 * 64:(c2 + 1) * 64], yn[:, c2 * 128:(c2 + 1) * 128], identf[0:64, 0:64])
                          for c2 in range(2):
                              P.ts(yT[1][:, c2, cs], pt_[:, c2 * 64:(c2 + 1) * 64], ssdng[:, c2:c2 + 1], ALU.mult)
                          yield
                          pSt = nb()
                          for h in range(4):
                              P.mm(pSt[:, h * 64:(h + 1) * 64], Bt[:], xdtd[:, h, :])
                          for h in range(4):
                              gr, hh = h // 2, h % 2
                              pr = slice(gr * 64, gr * 64 + 64)
                              P.stt(ssS[pr, hh, :], ssS[pr, hh, :], Eac[pr, h * 64 + 63:h * 64 + 64], pSt[pr, h * 64:(h + 1) * 64], ALU.mult, ALU.add)
                          P.copy(ssSb[:], ssS[:], e="act")
                          yield
                      if do_dump:
                          dump("ybT", yT[1][:], [128, 2, G])

                  def s5_gen(mt):
                      HL = 256
                      wre = mt("wre", [128, HL]); wim = mt("wim", [128, HL]); tt1 = mt("tt1", [128, HL]); tt2 = mt("tt2", [128, HL]); tt3 = mt("tt3", [128, HL]); tt4 = mt("tt4", [128, HL])
                      rre = mt("rre", [128, HL]); rim = mt("rim", [128, HL])
                      hre = mt("hre", [128, 2, HL], BF16, nslots=2); him = mt("him", [128, 2, HL], BF16, nslots=2)
                      y1 = mt("y1", [128, 2, G]); y1b = mt("y1b", [128, 2, G], BF16)
                      yv = mt("yv", [128, HL]); g1 = mt("g1", [128, HL]); sg = mt("sg", [128, G])
                      yield
                      glu_pan, glu_tr = panel(dap(W["s5_w_glu"], l * 65536, [[256, 128], [128 * 256, 2], [1, 256]]), [128, 2, 256])
                      for hf in range(2):
                          ts_ = slice(hf * HL, (hf + 1) * HL)
                          pyc = [nb(True), nb(True)]
                          for c in range(8):
                              pbr = nb(); pbi = nb()
                              P.mm(pbr[:, 0:HL], BBpad[:, c, 0, :], s5_uTb[:, c // 4, ts_])
                              P.mm(pbi[:, 0:HL], BBpad[:, c, 1, :], s5_uTb[:, c // 4, ts_])
                              yield
                              P.tt(wre[:], s5_Dre[:, c, :], pbr[:, 0:HL], ALU.mult)
                              P.tt(tt1[:], s5_Dim[:, c, :], pbi[:, 0:HL], ALU.mult)
                              P.tt(wim[:], s5_Dre[:, c, :], pbi[:, 0:HL], ALU.mult)
                              P.tt(tt2[:], s5_Dim[:, c, :], pbr[:, 0:HL], ALU.mult)
                              P.tt(wre[:], wre[:], tt1[:], ALU.subtract)
                              P.tt(wim[:], wim[:], tt2[:], ALU.add)
                              P.scan(rre[:], ones256[:], wre[:], s5carry[:, c, 0:1])
                              P.scan(rim[:], ones256[:], wim[:], s5carry[:, c, 1:2])
                              yield
                              s = c % 2
                              hr = hre.s(s, (slice(None), s, slice(None))); hi = him.s(s, (slice(None), s, slice(None)))
                              P.tt(tt1[:], s5_Ere[:, c, :], rre[:], ALU.mult)
                              P.tt(tt2[:], s5_Eim[:, c, :], rim[:], ALU.mult)
                              P.tt(tt3[:], s5_Ere[:, c, :], rim[:], ALU.mult)
                              P.tt(tt4[:], s5_Eim[:, c, :], rre[:], ALU.mult)
                              P.tt(hr, tt1[:], tt2[:], ALU.subtract)
                              P.tt(s5carry[:, c, 0:1], tt1[:, HL - 1:HL], tt2[:, HL - 1:HL], ALU.subtract)
                              P.tt(hi, tt3[:], tt4[:], ALU.add)
                              P.tt(s5carry[:, c, 1:2], tt3[:, HL - 1:HL], tt4[:, HL - 1:HL], ALU.add)
                              P.mm(pyc[c // 4][:, 0:HL], Cpad[:, c, 0, :], hr, start=(c % 4 == 0), stop=False)
                              P.mm(pyc[c // 4][:, 0:HL], Cpad[:, c, 1, :], hi, start=False, stop=(c % 4 == 3))
                              yield
                          for cc in range(2):
                              P.stt(yv[:], s5_uTb[:, cc, ts_], s5d[:, cc:cc + 1], pyc[cc][:, 0:HL], ALU.mult, ALU.add)
                              P.tt(g1[:], yv[:], yv[:], ALU.mult)
                              P.ts(g1[:], g1[:], 0.044715, ALU.mult, 1.0, ALU.add)
                              P.tt(g1[:], g1[:], yv[:], ALU.mult)
                              P.act(g1[:], g1[:], AF.Sigmoid, scale=1.5957691216057308)
                              P.tt(y1[:, cc, ts_], yv[:], g1[:], ALU.mult)
                              P.copy(y1b[:, cc, ts_], y1[:, cc, ts_], e="act")
                          release(*pyc)
                          if hf == 0:
                              yield "HALF"
                      for cc in range(2):
                          b_ = nb()
                          for kc in range(2):
                              P.mm(b_[:], V(glu_pan[:, kc, cc * 128:(cc + 1) * 128], glu_tr), y1b[:, kc, :], start=(kc == 0), stop=(kc == 1))
                          P.act(sg[:], b_[:], AF.Sigmoid)
                          P.tt(yT[2][:, cc, :], y1[:, cc, :], sg[:], ALU.mult)
                      if do_dump:
                          dump("ycT", yT[2][:], [128, 2, G])

                  def att_gen(mt):
                      qsq = mt("qsq", [128, 512]); qss = mt("qss", [128, 8]); qkn = mt("qkn", [128, 512])
                      qT = mt("qT", [128, 2, G], BF16)
                      pT = mt("pT", [128, 5, 512], BF16, nslots=5)
                      sS = mt("sS", [128, 2, 512], F32, nslots=2)
                      rs = mt("rs", [128, 4]); on_ = mt("aon", [128, 256])
                      for i in range(4):
                          gi_ = (g % SEQG) * 4 + i
                          slot = gi_ % 8
                          ts_ = slice(i * 128, (i + 1) * 128)
                          P.act(qsq[:], at_qk[:, i, :], AF.Square)
                          P.op("dve", lambda g_: g_.tensor_reduce(qss.t[:], qsq.t[:].rearrange("p (h d) -> p h d", h=8), AX.X, ALU.add),
                               [qsq[:]], [qss[:]])
                          P.ts(qss[:], qss[:], 1.0 / 64, ALU.mult, EPS, ALU.add)
                          P.act(qss[:], qss[:], AF.Sqrt)
                          P.recip(qss[:], qss[:])
                          P.tt(V(qkn.t[:].rearrange("p (h d) -> p h d", h=8), qkn.trks),
                               V(at_qk.t[:, i, :].rearrange("p (h d) -> p h d", h=8), at_qk.trks),
                               V(qss.t[:].unsqueeze(2).broadcast_to([128, 8, 64]), qss.trks), ALU.mult)
                          pt_ = nb()
                          for blk in range(4):
                              P.tr(pt_[:, blk * 128:(blk + 1) * 128], qkn[:, blk * 128:(blk + 1) * 128], identf[:])
                          for c2 in range(2):
                              P.ts(qT[:, c2, ts_], pt_[:, c2 * 128:(c2 + 1) * 128], gqk[:, 0:1], ALU.mult)
                              P.copy(kTh[:, c2, slot * 128:(slot + 1) * 128], pt_[:, (2 + c2) * 128:(3 + c2) * 128], e="act")
                          yield
                          kts = [kt for kt in range(gi_ - 4, gi_ + 1) if kt >= 0]
                          for j, kt in enumerate(kts):
                              dl = gi_ - kt
                              ks = kt % 8
                              ps_ = nb()
                              for h in range(4):
                                  pr = slice((h % 2) * 64, (h % 2) * 64 + 64)
                                  P.mm(ps_[:, h * 128:(h + 1) * 128], kTh[pr, h // 2, ks * 128:(ks + 1) * 128], qT[pr, h // 2, ts_])
                              sv = sS.s(j % 2, (slice(None), j % 2, slice(None)))
                              P.tt(sv, ps_[:], V(biasM.t[:, dl, :, :].rearrange("p h q -> p (h q)"), biasM.trks), ALU.add)
                              P.act(pT.s(j, (slice(None), j, slice(None))), sv, AF.Exp)
                              yield
                          po = nb()
                          for h in range(4):
                              for j, kt in enumerate(kts):
                                  ks = kt % 8
                                  P.mm(po[:, h * 65:(h + 1) * 65], V(pT.t[:, j, h * 128:(h + 1) * 128], [pT.trks[j]]), Vaug[:, ks, h, 0:65],
                                       start=(j == 0), stop=(j == len(kts) - 1))
                          yield
                          P.recip(rs[:], V(po.t[:, 64:260:65], po.trks))
                          P.tt(V(on_.t[:].rearrange("p (h d) -> p h d", h=4), on_.trks),
                               V(po.t[:, 0:260].rearrange("p (h d) -> p h d", h=4)[:, :, 0:64], po.trks),
                               V(rs.t[:].unsqueeze(2).broadcast_to([128, 4, 64]), rs.trks), ALU.mult)
                          pt2 = nb()
                          for c2 in range(2):
                              P.tr(pt2[:, c2 * 128:(c2 + 1) * 128], on_[:, c2 * 128:(c2 + 1) * 128], identf[:])
                          P.copy(yT[3][:, :, ts_], V(pt2.t[:, 0:256].rearrange("p (c t) -> p c t", c=2), pt2.trks), e="act")
                          yield
                      if do_dump:
                          dump("ydT", yT[3][:], [128, 2, G])
                          dump("biasM", biasM[:], [128, 5, 4, 128])
                          dump("qT", qT[:], [128, 2, G])
                          dump("kTh", kTh[:], [128, 2, 1024])
                          dump("Vaug", Vaug[:], [128, 8, 4, 72])
                          dump("pT", pT[:], [128, 5, 512])
                          dump("aon", on_[:], [128, 256])

                  def make_mt(st):
                      def mt(name, shape, dt=F32, nslots=1, _st=st):
                          t = _st.enter_context(nc.sbuf_tensor(f"{name}_{l}_{g}", list(shape), dt))
                          return Tile(P, t, name, nslots)
                      return mt

                  def gates_gen(mt):
                      for m in range(4):
                          gpan, gtr = wpanel_rows(win, l, 0, 8, 2820 + m * 1024, 512)
                          for jj in range(4):
                              pg = nb()
                              for kc in range(8):
                                  P.mm(pg[:], V(gpan[:, kc, jj * 128:(jj + 1) * 128], gtr), hT[:, kc, :], start=(kc == 0), stop=(kc == 7), last=(kc == 7))
                              P.act(sgT[:, m * 4 + jj, :], pg[:], AF.Sigmoid)
                              yield

                  def run_group(gens):
                      with ExitStack() as mst:
                          mt = make_mt(mst)
                          alive = [g_(mt) for g_ in gens]
                          while alive:
                              for g_ in list(alive):
                                  try:
                                      next(g_)
                                  except StopIteration:
                                      alive.remove(g_)
                          barrier()
                  run_group([hgrn_gen, s5_gen])
                  run_group([ssd_gen, att_gen, gates_gen])

              chk('att')
              with ExitStack() as gst:
                  def gt(name, shape, dt=F32, nslots=1, _st=gst):
                      t = _st.enter_context(nc.sbuf_tensor(f"{name}_{l}_{g}", list(shape), dt))
                      return Tile(P, t, name, nslots)
                  sig = gt("sig", [128, 2, G], F32, nslots=2)
                  mergedT = gt("mergedT", [128, 8, G], BF16)
                  acc4 = gt("acc4", [128, 4, G], F32, nslots=4)
                  tmpm = gt("tmpm", [128, 2, G], F32, nslots=2)
                  wbr = gt("wbr", [128, 4, 2, 1024], BF16)
                  wbd = W["w_branch"]
                  for m_ in range(4):
                      P.dma("pool", wbr.t[:, m_, :, :], dap(wbd, l * 4 * 256 * 1024 + m_ * 256 * 1024, [[1024, 128], [128 * 1024, 2], [1, 1024]]),
                            "wbr", out_v=wbr[:])
                  cnt_ = 0
                  for jq in range(2):
                      for m in range(4):
                          if jq == 1:
                              gpan, gtr = wpanel_rows(win, l, 0, 8, 2820 + m * 1024 + jq * 512, 512)
                          for jj in range(4):
                              j = jq * 4 + jj
                              pbm = nb()
                              if jq == 1:
                                  pg = nb()
                                  for kc in range(8):
                                      P.mm(pg[:], V(gpan[:, kc, jj * 128:(jj + 1) * 128], gtr), hT[:, kc, :], start=(kc == 0), stop=(kc == 7), last=(kc == 7))
                              for kc in range(2):
                                  P.mm(pbm[:], wbr[:, m, kc, j * 128:(j + 1) * 128], yT[m][:, kc, :], start=(kc == 0), stop=(kc == 1), last=(kc == 1))
                              s_ = cnt_ % 2
                              cnt_ += 1
                              sv = sig.s(s_, (slice(None), s_, slice(None)))
                              av = acc4.s(jj, (slice(None), jj, slice(None)))
                              tv = tmpm.s(s_, (slice(None), s_, slice(None)))
                              if jq == 1:
                                  P.act(sv, pg[:], AF.Sigmoid)
                              else:
                                  sv = sgT[:, m * 4 + jj, :]
                              if m == 0:
                                  P.tt(av, sv, pbm[:], ALU.mult)
                              else:
                                  P.tt(tv, sv, pbm[:], ALU.mult)
                                  if m < 3:
                                      P.tt(av, av, tv, ALU.add)
                                  else:
                                      P.tt(mergedT[:, j, :], av, tv, ALU.add)
                  if do_dump:
                      dump("mergedT", mergedT[:], [128, 8, G])
                  for pj in range(2):
                      pan, ptr = wpanel_rows(W["w_out"], l, 0, 8, pj * 512, 512)
                      for jj in range(4):
                          b_ = nb()
                          lin_b(b_, pan, ptr, slice(jj * 128, (jj + 1) * 128), lambda kc: mergedT[:, kc, :], 8)
                          P.tt(xT[:, pj * 4 + jj, :], xT[:, pj * 4 + jj, :], b_[:], ALU.add)
                  if do_dump:
                      dump("x1T", xT[:], [128, 8, G])
                  chk('merge')
                  norm(1)
                  aT = gt("aT", [128, 32, G], BF16)
                  rt = gt("rt", [128, 2, G], BF16, nslots=2)
                  for pj in range(8):
                      pan, ptr = wpanel_rows(W["w_ff1"], l, 0, 8, pj * 512, 512)
                      for jj in range(4):
                          b_ = nb()
                          lin_b(b_, pan, ptr, slice(jj * 128, (jj + 1) * 128), lambda kc: hT[:, kc, :], 8)
                          s = jj % 2
                          rv = rt.s(s, (slice(None), s, slice(None)))
                          P.act(rv, b_[:], AF.Relu)
                          P.tt(aT[:, pj * 4 + jj, :], rv, rv, ALU.mult)
                  for hf in range(2):
                      accb = [nb(True) for _ in range(4)]
                      for kq in range(4):
                          pan, ptr = wpanel_rows(W["w_ff2"], l, kq * 1024, 8, hf * 512, 512)
                          for jj in range(4):
                              for kc in range(8):
                                  P.mm(accb[jj][:], V(pan[:, kc, jj * 128:(jj + 1) * 128], ptr), aT[:, kq * 8 + kc, :],
                                       start=(kq == 0 and kc == 0), stop=(kq == 3 and kc == 7), last=(kc == 7))
                      for jj in range(4):
                          P.tt(xT[:, hf * 4 + jj, :], xT[:, hf * 4 + jj, :], accb[jj][:], ALU.add)
                      release(*accb)
                  if do_dump:
                      dump("x2T", xT[:], [128, 8, G])
                  barrier()
              sg_st.close()
              chk('ffn')
              with ExitStack() as gst:
                  def gt(name, shape, dt=F32, nslots=1, _st=gst):
                      t = _st.enter_context(nc.sbuf_tensor(f"{name}_{l}_{g}", list(shape), dt))
                      return Tile(P, t, name, nslots)
                  norm(2)
                  p_tok = gt("p_tok", [128, 2, 256], F32, nslots=2)
                  pT_ = gt("pT_", [128, 2, G], BF16)
                  sgp = gt("sgp", [128, 2, G], F32, nslots=2)
                  x_out = gt("x_out", [128, 2, 1024], F32, nslots=2)
                  for i in range(4):
                      s = i % 2
                      P.dma("sp", p_tok.t[:, s, :], p_d[l, tok0 + i * 128: tok0 + (i + 1) * 128, :], f"pl{s}",
                            out_v=p_tok.s(s, (slice(None), s, slice(None))))
                      b_ = nb()
                      for c2 in range(2):
                          P.tr(b_[:, c2 * 128:(c2 + 1) * 128], p_tok.s(s, (slice(None), s, slice(c2 * 128, (c2 + 1) * 128))), identf[:])
                      P.copy(pT_[:, :, i * 128:(i + 1) * 128], V(b_.t[:, 0:256].rearrange("p (c t) -> p c t", c=2), b_.trks), e="act")
                  plew = gt("plew", [128, 2, 1024], BF16)
                  P.dma("pool", plew.t[:], dap(W["w_ple"], l * 256 * 1024, [[1024, 128], [128 * 1024, 2], [1, 1024]]), "plew", out_v=plew[:])
                  plepan, pletr = plew.t, plew.trks
                  for pj in range(2):
                      pan, ptr = wpanel_rows(W["w_ple_gate"], l, 0, 8, pj * 512, 512)
                      for jj in range(4):
                          j = pj * 4 + jj
                          pg = nb(); pw = nb()
                          lin_b(pg, pan, ptr, slice(jj * 128, (jj + 1) * 128), lambda kc: hT[:, kc, :], 8)
                          for kc in range(2):
                              P.mm(pw[:], V(plepan[:, kc, j * 128:(j + 1) * 128], pletr), pT_[:, kc, :], start=(kc == 0), stop=(kc == 1), last=(kc == 1))
                          s = jj % 2
                          sv = sgp.s(s, (slice(None), s, slice(None)))
                          P.act(sv, pg[:], AF.Sigmoid)
                          P.tt(sv, sv, pw[:], ALU.mult)
                          P.tt(xT[:, j, :], xT[:, j, :], sv, ALU.add)
                  if do_dump:
                      dump("x3T", xT[:], [128, 8, G])
                  if l == n_layers - 1:
                      for i in range(4):
                          s = i % 2
                          for hf in range(2):
                              b_ = nb()
                              for q in range(4):
                                  P.tr(b_[:, q * 128:(q + 1) * 128], xT[:, hf * 4 + q, i * 128:(i + 1) * 128], identf[:])
                              P.copy(V(x_out.t[:, s, hf * 512:(hf + 1) * 512], [x_out.trks[s]]), b_[:], e="act")
                          ev = P.dma("sp", x_dst[tok0 + i * 128: tok0 + (i + 1) * 128, :], x_out.t[:, s, :], f"xst{s}",
                                     in_v=x_out.s(s, (slice(None), s, slice(None))))
                          P.out_events.append(ev)
                      for ev in P.out_events[-4:]:
                          P.wait_event("sp", ev)

                  else:
                      ev = P.dma("sp", dap(xs1, tok0, [[T, 128], [128 * T, 8], [1, G]]), xT.t[:], "xsT", in_v=xT[:])
                      P.wait_event("sp", ev)
                  barrier()

    except StopBuild:
        pass
    P.finish()
    return P, dbg_out


dbg_l = 0
dbg_g = 0
_CACHE = {}


def kernel(**inputs):
    x = np.ascontiguousarray(np.asarray(inputs["x"], dtype=np.float32))
    p = np.ascontiguousarray(np.asarray(inputs["p"], dtype=np.float32))
    if "prog" not in _CACHE:
        _CACHE["prog"] = build()
    P, _ = _CACHE["prog"]
    in_maps = []
    for c in range(NCORES):
        m = {"x": x[2 * c:2 * c + 2].reshape(T, D), "p": p[:, 2 * c:2 * c + 2].reshape(2, T, 256)}
        for n, s in WNAMES:
            m[n] = np.ascontiguousarray(np.asarray(inputs[n], dtype=np.float32))
        in_maps.append(m)
    res = run_bass_kernel_spmd(P.nc, in_maps, core_ids=list(range(NCORES)))
    out = np.concatenate([r["y"].reshape(2, 2048, D) for r in res.results], axis=0)
    return out.astype(np.float32)
```
